# Optimizing a Trainium2 kernel written in Bass

```python
import jax, jax.numpy as jnp
from jax import lax
import numpy as np

D_MODEL = 2048
BATCH = 8
SEQ = 2048
DEPTH = 1

DN_HEADS = 8
DN_HEAD_DIM = 128
DN_WIDTH = DN_HEADS * DN_HEAD_DIM
DN_CONV = 4
DN_CHUNK = 64
CF_WIDTH = 1024
CF_KERNEL = 31
FFN_DIM = 5632
FFN_CONV = 3
EPS = 1e-6

IN_SIZES = [3 * DN_WIDTH,
            DN_WIDTH,
            DN_HEADS,
            DN_HEADS,
            2 * CF_WIDTH,
            D_MODEL,
            D_MODEL]
N_IN = sum(IN_SIZES)
IN_SPLITS = [int(v) for v in np.cumsum(IN_SIZES)[:-1]]

kernel_name = "hybrid_deltanet_conformer_convffn_adaln"


def rmsnorm(x, g):
    xf = x.astype(jnp.float32)
    y = xf * lax.rsqrt(jnp.mean(xf * xf, axis=-1, keepdims=True) + EPS)
    return (y * g.astype(jnp.float32)).astype(x.dtype)


def layernorm(x, g, b):
    xf = x.astype(jnp.float32)
    mu = jnp.mean(xf, axis=-1, keepdims=True)
    xc = xf - mu
    y = xc * lax.rsqrt(jnp.mean(xc * xc, axis=-1, keepdims=True) + EPS)
    return (y * g.astype(jnp.float32) + b.astype(jnp.float32)).astype(x.dtype)


def l2norm(x):
    return x * lax.rsqrt(jnp.sum(x * x, axis=-1, keepdims=True) + EPS)


def causal_dwconv(x, w):
    k = w.shape[0]
    return lax.conv_general_dilated(
        x, w[:, None, :].astype(x.dtype), window_strides=(1,), padding=[(k - 1, 0)],
        dimension_numbers=("NWC", "WIO", "NWC"), feature_group_count=x.shape[-1])


def chunk_gated_delta_rule(q, k, v, g, beta):
    b, s, h, dk = q.shape
    dv = v.shape[-1]
    c = DN_CHUNK
    n = s // c

    def to_chunks(t):
        return jnp.moveaxis(t.reshape((b, n, c, h) + t.shape[3:]), 3, 1)

    q, k, v, g, beta = (to_chunks(t) for t in (q, k, v, g, beta))
    g = jnp.cumsum(g, axis=-1)
    causal = jnp.tril(jnp.ones((c, c), dtype=bool))
    strict = jnp.tril(jnp.ones((c, c), dtype=bool), -1)
    diff = g[..., :, None] - g[..., None, :]
    decay = jnp.where(causal, jnp.exp(jnp.where(causal, diff, 0.0)), 0.0)

    kb = k * beta[..., None]
    m = jnp.einsum("bhnid,bhnjd->bhnij", kb, k) * decay
    m = jnp.where(strict, m, 0.0) + jnp.eye(c, dtype=jnp.float32)
    rhs = jnp.concatenate([v * beta[..., None], kb * jnp.exp(g)[..., None]], axis=-1)
    sol = lax.linalg.triangular_solve(m, rhs, left_side=True, lower=True, unit_diagonal=True)
    u0, w = sol[..., :dv], sol[..., dv:]

    attn = jnp.einsum("bhnid,bhnjd->bhnij", q, k) * decay
    q_dec = q * jnp.exp(g)[..., None]
    g_last = g[..., -1]
    k_dec = k * jnp.exp(g_last[..., None] - g)[..., None]

    def step(state, inp):
        u0_i, w_i, attn_i, qd_i, kd_i, gl_i = inp
        u = u0_i - jnp.einsum("bhck,bhkv->bhcv", w_i, state)
        o = (jnp.einsum("bhck,bhkv->bhcv", qd_i, state)
             + jnp.einsum("bhij,bhjv->bhiv", attn_i, u))
        state = (state * jnp.exp(gl_i)[..., None, None]
                 + jnp.einsum("bhck,bhcv->bhkv", kd_i, u))
        return state, o

    xs = tuple(jnp.moveaxis(t, 2, 0) for t in (u0, w, attn, q_dec, k_dec, g_last))
    s0 = jnp.zeros((b, h, dk, dv), jnp.float32)
    _, o = lax.scan(step, s0, xs)
    o = jnp.moveaxis(o, 0, 2)
    return jnp.moveaxis(o, 1, 3).reshape(b, s, h, dv)


def hybrid_mixer(hn, w_in, dn_conv_w, dn_a_log, dn_dt_bias, dn_norm_g, dn_w_o,
                 cf_conv_w, cf_ln_g, cf_ln_b, cf_w_o, w_out):
    b, s, _ = hn.shape
    proj = hn @ w_in
    qkv, z, beta_logit, a_logit, glu_in, gate_a, gate_b = jnp.split(proj, IN_SPLITS, axis=-1)

    qkv = jax.nn.silu(causal_dwconv(qkv, dn_conv_w)).astype(jnp.float32)
    q, k, v = (t.reshape(b, s, DN_HEADS, DN_HEAD_DIM) for t in jnp.split(qkv, 3, axis=-1))
    q = l2norm(q) * (DN_HEAD_DIM ** -0.5)
    k = l2norm(k)
    beta = jax.nn.sigmoid(beta_logit.astype(jnp.float32))
    g = -jnp.exp(dn_a_log.astype(jnp.float32)) * jax.nn.softplus(
        a_logit.astype(jnp.float32) + dn_dt_bias.astype(jnp.float32))
    o = chunk_gated_delta_rule(q, k, v, g, beta)
    o = o * lax.rsqrt(jnp.mean(o * o, axis=-1, keepdims=True) + EPS) * dn_norm_g.astype(jnp.float32)
    o = o * jax.nn.silu(z.astype(jnp.float32).reshape(b, s, DN_HEADS, DN_HEAD_DIM))
    branch_a = o.reshape(b, s, DN_WIDTH).astype(hn.dtype) @ dn_w_o

    val, gl = jnp.split(glu_in, 2, axis=-1)
    u = val * jax.nn.sigmoid(gl)
    u = causal_dwconv(u, cf_conv_w)
    u = jax.nn.silu(layernorm(u, cf_ln_g, cf_ln_b))
    branch_b = u @ cf_w_o

    merged = jax.nn.sigmoid(gate_a) * branch_a + jax.nn.sigmoid(gate_b) * branch_b
    return merged @ w_out


def conv_glu_ffn(hn, w_up, conv_w, w_down):
    gate, up = jnp.split(hn @ w_up, 2, axis=-1)
    gate = causal_dwconv(gate, conv_w)
    return (jax.nn.silu(gate) * up) @ w_down


def setup_inputs(seed: int = 0) -> dict:
    key = jax.random.key(seed)
    ks = jax.random.split(key, 24)
    L, D = DEPTH, D_MODEL
    nrm = lambda k, shape, s: jax.random.normal(k, shape, jnp.float32) * s
    dt = jnp.exp(jax.random.uniform(ks[5], (L, DN_HEADS), jnp.float32,
                                    float(np.log(1e-3)), float(np.log(1e-1))))
    return {
        "x": nrm(ks[0], (BATCH, SEQ, D), 1.0),
        "c": nrm(ks[1], (BATCH, D), 1.0),
        "w_ada": nrm(ks[2], (L, D, 6 * D), 0.5 * D ** -0.5),
        "b_ada": nrm(ks[3], (L, 6 * D), 0.01),
        "norm1_g": 1.0 + nrm(ks[4], (L, D), 0.02),
        "w_in": nrm(ks[6], (L, D, N_IN), D ** -0.5),
        "dn_conv_w": nrm(ks[7], (L, DN_CONV, 3 * DN_WIDTH), DN_CONV ** -0.5),
        "dn_a_log": jnp.log(jax.random.uniform(ks[8], (L, DN_HEADS), jnp.float32, 1.0, 16.0)),
        "dn_dt_bias": jnp.log(jnp.expm1(dt)),
        "dn_norm_g": 1.0 + nrm(ks[9], (L, DN_HEAD_DIM), 0.02),
        "dn_w_o": nrm(ks[10], (L, DN_WIDTH, D), DN_WIDTH ** -0.5),
        "cf_conv_w": nrm(ks[11], (L, CF_KERNEL, CF_WIDTH), CF_KERNEL ** -0.5),
        "cf_ln_g": 1.0 + nrm(ks[12], (L, CF_WIDTH), 0.02),
        "cf_ln_b": nrm(ks[13], (L, CF_WIDTH), 0.02),
        "cf_w_o": nrm(ks[14], (L, CF_WIDTH, D), CF_WIDTH ** -0.5),
        "w_out": nrm(ks[15], (L, D, D), D ** -0.5),
        "norm2_g": 1.0 + nrm(ks[16], (L, D), 0.02),
        "ffn_w_up": nrm(ks[17], (L, D, 2 * FFN_DIM), D ** -0.5),
        "ffn_conv_w": nrm(ks[18], (L, FFN_CONV, FFN_DIM), FFN_CONV ** -0.5),
        "ffn_w_down": nrm(ks[19], (L, FFN_DIM, D), FFN_DIM ** -0.5),
        "final_norm_g": 1.0 + nrm(ks[20], (D,), 0.02),
    }


def reference(x, c, w_ada, b_ada, norm1_g, w_in, dn_conv_w, dn_a_log, dn_dt_bias, dn_norm_g,
              dn_w_o, cf_conv_w, cf_ln_g, cf_ln_b, cf_w_o, w_out, norm2_g, ffn_w_up, ffn_conv_w,
              ffn_w_down, final_norm_g):
    c_act = jax.nn.silu(c)
    for l in range(DEPTH):
        mod = c_act @ w_ada[l] + b_ada[l]
        sh1, sc1, gt1, sh2, sc2, gt2 = jnp.split(mod[:, None, :], 6, axis=-1)
        hn = rmsnorm(x, norm1_g[l]) * (1.0 + sc1) + sh1
        x = x + gt1 * hybrid_mixer(hn, w_in[l], dn_conv_w[l], dn_a_log[l], dn_dt_bias[l],
                                   dn_norm_g[l], dn_w_o[l], cf_conv_w[l], cf_ln_g[l],
                                   cf_ln_b[l], cf_w_o[l], w_out[l])
        hn = rmsnorm(x, norm2_g[l]) * (1.0 + sc2) + sh2
        x = x + gt2 * conv_glu_ffn(hn, ffn_w_up[l], ffn_conv_w[l], ffn_w_down[l])
    return rmsnorm(x, final_norm_g)
```

```python
import numpy as np
from contextlib import ExitStack
import concourse.bass as bass
import concourse.mybir as mybir
from concourse.bass_utils import run_bass_kernel_spmd

F32 = mybir.dt.float32
BF16 = mybir.dt.bfloat16
AF = mybir.ActivationFunctionType
ALU = mybir.AluOpType

D = 2048
KC = D // 128
H = 8
EPS = 1e-6

RESCHED = True
EPOCH = 6000
DMA_EPOCH = 1500
N_DMA_SEMS = 14


class Buf:
    __slots__ = ("name", "last_w", "readers", "psum")

    def __init__(self, name, psum=False):
        self.name = name
        self.psum = psum
        self.last_w = None
        self.readers = []


class Op:
    __slots__ = ("eng", "fn", "deps", "need_mark", "mark", "is_dma", "prev_same_sem", "odeps", "cost", "gi")

    def __init__(self, eng, fn, is_dma):
        self.eng = eng
        self.fn = fn
        self.odeps = []
        self.cost = 300.0
        self.gi = 0
        self.deps = []
        self.need_mark = is_dma
        self.mark = None
        self.is_dma = is_dma
        self.prev_same_sem = None


class Sched:
    ENGS = ("pe", "act", "dve", "pool", "sp")

    def __init__(self):
        self.ops = {e: [] for e in self.ENGS}
        self.all_dma = []
        self.out_dma = []
        self.glob = []

    def add(self, eng, fn, r=(), w=(), dma=False, cost=300.0):
        op = Op(eng, fn, dma)
        op.cost = cost
        op.gi = len(self.glob)
        self.glob.append(op)
        deps = []
        odeps = op.odeps
        seen = set()
        w = list(w) + [b for b in r if b.psum]
        r = [b for b in r if not b.psum]

        def dep(o):
            if o is None or id(o) in seen:
                return
            if o.eng == "pe" and eng == "pe" and not o.is_dma and not dma:
                seen.add(id(o))
                odeps.append(o)
                return
            seen.add(id(o))
            deps.append(o)

        for b in r:
            dep(b.last_w)
        for b in w:
            dep(b.last_w)
            for o in b.readers:
                dep(o)
        for o in deps:
            o.need_mark = True
        op.deps = deps
        for b in r:
            b.readers.append(op)
        for b in w:
            b.last_w = op
            b.readers = []
        self.ops[eng].append(op)
        if dma:
            self.all_dma.append(op)
        return op

    def dma(self, q, out, in_, r=(), w=()):
        n = 1
        for d in out.shape:
            n *= d
        return self.add(q, lambda e: e.dma_start(out=out, in_=in_), r, w, dma=True, cost=2000.0 + n * 4 / 250.0)

    def reschedule(self):
        import heapq
        ops = self.glob
        n = len(ops)
        import os as _os
        LAT = float(_os.environ.get("K_LAT", "350"))
        succ = [[] for _ in range(n)]
        indeg = [0] * n
        for op in ops:
            for d in op.deps + op.odeps:
                succ[d.gi].append(op.gi)
                indeg[op.gi] += 1
        blevel = [0.0] * n
        for i in range(n - 1, -1, -1):
            op = ops[i]
            b = 0.0
            for sidx in succ[i]:
                v = blevel[sidx] + (0.0 if (ops[sidx].eng == op.eng and not op.is_dma) else LAT)
                if v > b:
                    b = v
            blevel[i] = b + op.cost
        fut = {e: [] for e in self.ENGS}
        avl = {e: [] for e in self.ENGS}
        efree = {e: 0.0 for e in self.ENGS}
        finish = [0.0] * n
        dready = [0.0] * n
        for i in range(n):
            if indeg[i] == 0:
                heapq.heappush(fut[ops[i].eng], (0.0, -blevel[i], i))
        neworder = {e: [] for e in self.ENGS}
        why = [None] * n
        startt = [0.0] * n
        lastop = {e: None for e in self.ENGS}
        drsrc = [None] * n
        done = 0
        while done < n:
            best = None
            for e in self.ENGS:
                t = efree[e]
                if avl[e]:
                    cand_t = t
                elif fut[e]:
                    cand_t = max(t, fut[e][0][0])
                else:
                    continue
                if best is None or cand_t < best[0]:
                    best = (cand_t, e)
            t, e = best
            while fut[e] and fut[e][0][0] <= t:
                dr, nb, gi = heapq.heappop(fut[e])
                heapq.heappush(avl[e], (nb, gi))
            nb, gi = heapq.heappop(avl[e])
            op = ops[gi]
            start = max(t, dready[gi])
            startt[gi] = start
            if dready[gi] >= efree[e] and drsrc[gi] is not None:
                why[gi] = ("dep", drsrc[gi])
            elif lastop[e] is not None:
                why[gi] = ("eng", lastop[e])
            lastop[e] = gi
            if op.is_dma:
                efree[e] = start + 120.0
            else:
                efree[e] = start + op.cost
            finish[gi] = start + op.cost
            neworder[e].append(op)
            done += 1
            for sidx in succ[gi]:
                so = ops[sidx]
                lat = 0.0 if (so.eng == op.eng and not op.is_dma) else LAT
                v = finish[gi] + lat
                if v > dready[sidx]:
                    dready[sidx] = v
                    drsrc[sidx] = gi
                indeg[sidx] -= 1
                if indeg[sidx] == 0:
                    heapq.heappush(fut[so.eng], (dready[sidx], -blevel[sidx], sidx))
        self.ops = neworder
        import os as _os
        if _os.environ.get("K_CRIT"):
            cur = max(range(n), key=lambda i: finish[i])
            acc = {}
            hops = 0
            segs = []
            while cur is not None:
                op = ops[cur]
                k = op.eng + ("_dma" if op.is_dma else "")
                acc[k] = acc.get(k, 0.0) + op.cost
                segs.append((startt[cur], k, op.cost))
                w_ = why[cur]
                if w_ is None:
                    break
                if w_[0] == "dep":
                    hops += 1
                cur = w_[1]
            print("critical path: by engine us", {k: round(v / 1e3, 1) for k, v in acc.items()}, "dep hops", hops)
            import collections
            win = collections.defaultdict(lambda: collections.defaultdict(float))
            for st_, k, c in segs:
                win[int(st_ // 500000)][k] += c
            for wi in sorted(win):
                print("  t=%5.1fms" % (wi * 0.5), {k: round(v / 1e3) for k, v in win[wi].items()})
        return max(finish) if n else 0.0

    def count(self):
        return {e: len(v) for e, v in self.ops.items()}

    def emit(self, nc, stack, final_wait_ops=()):
        def new_sem(name):
            return stack.enter_context(nc.semaphore(name))

        for e in ("pe", "act", "dve", "pool"):
            cnt = 0
            sem = None
            k = 0
            for op in self.ops[e]:
                if op.is_dma or not op.need_mark:
                    continue
                if sem is None or cnt >= EPOCH:
                    sem = new_sem(f"s_{e}_{k}")
                    k += 1
                    cnt = 0
                cnt += 1
                op.mark = (sem, cnt)
        for q in ("sp", "pool"):
            dmas = [op for op in self.ops[q] if op.is_dma]
            slots = [[None, 0, None, 0] for _ in range(N_DMA_SEMS)]
            for i, op in enumerate(dmas):
                sl = slots[i % N_DMA_SEMS]
                if sl[0] is None or sl[1] >= DMA_EPOCH:
                    sl[0] = new_sem(f"d_{q}_{i % N_DMA_SEMS}_{sl[3]}")
                    sl[3] += 1
                    sl[1] = 0
                    sl[2] = None
                sl[1] += 1
                op.mark = (sl[0], 16 * sl[1])
                op.prev_same_sem = sl[2]
                sl[2] = op

        final_wait_ops = list(final_wait_ops)
        engmap = {"pe": "tensor", "act": "scalar", "dve": "vector", "pool": "gpsimd", "sp": "sync"}
        sched = self

        def run_engine(ename):
            def body(e):
                waited = {}

                def wait_for(o):
                    sem, val = o.mark
                    key = id(sem)
                    if waited.get(key, 0) >= val:
                        return
                    e.wait_ge(sem, val)
                    waited[key] = val

                for op in sched.ops[ename]:
                    for d in op.deps:
                        wait_for(d)
                    if op.is_dma and op.prev_same_sem is not None:
                        wait_for(op.prev_same_sem)
                    ins = op.fn(e)
                    if op.mark is not None:
                        ins.then_inc(op.mark[0], 16 if op.is_dma else 1)
                if ename == "sp":
                    for o in final_wait_ops:
                        wait_for(o)
            return body

        with nc.Block() as block:
            for ename in self.ENGS:
                getattr(block, engmap[ename])(run_engine(ename))


class Arena:
    def __init__(self, nc, stack, nbytes):
        self.nbytes = nbytes
        self.t = stack.enter_context(nc.sbuf_tensor("arena", [128, nbytes // 4], F32))
        self.live = {}
        self.hist = []
        self.peak = 0

    def _find(self, n):
        spans = sorted((s, e) for s, e, _ in self.live.values())
        pos = 0
        for s, e in spans:
            if s - pos >= n:
                return pos
            pos = max(pos, e)
        if self.nbytes - pos >= n:
            return pos
        raise MemoryError(f"arena full: need {n}, live={sorted((v[0], v[1], k) for k, v in self.live.items())}")

    def alloc(self, name, nbytes, nbufs=1):
        nbytes = (nbytes + 63) // 64 * 64
        s = self._find(nbytes)
        e = s + nbytes
        self.peak = max(self.peak, e)
        bufs = [Buf(f"{name}{i}") for i in range(nbufs)]
        inh = []
        keep = []
        for (hs, he, hb) in self.hist:
            if hs < e and s < he:
                inh.extend(hb)
                if hs < s or he > e:
                    keep.append((hs, he, hb))
            else:
                keep.append((hs, he, hb))
        self.hist = keep
        for b in bufs:
            for hb in inh:
                if hb.last_w is not None:
                    b.readers.append(hb.last_w)
                b.readers.extend(hb.readers)
        assert name not in self.live, name
        self.live[name] = (s, e, bufs)
        return s, bufs

    def free(self, name):
        s, e, bufs = self.live.pop(name)
        self.hist.append((s, e, bufs))

    def view(self, off, ncols, dtype):
        assert off % 4 == 0
        if dtype == F32:
            return self.t[:, off // 4: off // 4 + ncols]
        assert ncols % 2 == 0
        return self.t[:, off // 4: off // 4 + ncols // 2].bitcast(BF16)


class Tile:
    def __init__(self, arena, name, ncols, dtype, nbufs=1):
        self.arena = arena
        self.name = name
        sz = 4 if dtype == F32 else 2
        self.off, self.bufs = arena.alloc(name, ncols * sz, nbufs)
        self.ap = arena.view(self.off, ncols, dtype)
        self.b = self.bufs[0]

    def free(self):
        self.arena.free(self.name)


def _relayout(W, gw):
    K, N = W.shape
    kc = K // 128
    g = N // gw
    return np.ascontiguousarray(W.reshape(kc, 128, g, gw).transpose(2, 1, 0, 3).reshape(g, 128, kc * gw))


def _pervec(v):
    return np.ascontiguousarray(v.reshape(-1, 128).T)


class _Cols:
    def __init__(self):
        self.n = 0
        self.off = {}
        self.parts = []

    def add(self, name, arr):
        arr = np.asarray(arr, np.float32)
        if arr.shape[0] < 128:
            arr = np.concatenate([arr, np.zeros((128 - arr.shape[0], arr.shape[1]), np.float32)], 0)
        self.off[name] = (self.n, arr.shape[1])
        self.n += arr.shape[1]
        self.parts.append(arr)

    def build(self):
        return np.ascontiguousarray(np.concatenate(self.parts, 1))


def _const_cols(S):
    C = _Cols()
    C.add("ident", np.eye(128, dtype=np.float32))
    x = np.arange(64)[:, None]
    y = np.arange(64)[None, :]
    C.add("sgn", np.where(y >= x, 1.0, -1.0))
    C.add("mge", (y >= x).astype(np.float32))
    C.add("mlt", (y < x).astype(np.float32))
    C.add("mgt", (y > x).astype(np.float32))
    sel64 = np.zeros((8, 8 * 64), np.float32)
    sel128 = np.zeros((8, 8 * 128), np.float32)
    for h in range(8):
        sel64[h, h * 64:(h + 1) * 64] = 1
        sel128[h, h * 128:(h + 1) * 128] = 1
    C.add("sel64", sel64)
    C.add("sel128", sel128)
    return C


def _seg(S):
    seg = np.ones((8, S), np.float32)
    seg[:, ::64] = 0
    return seg


def _param_cols(inp, b, FF):
    P = _Cols()
    P.add("c", _pervec(inp["c"][b]))
    P.add("b_ada", _pervec(inp["b_ada"][0]))
    P.add("g1", _pervec(inp["norm1_g"][0]))
    P.add("g2", _pervec(inp["norm2_g"][0]))
    P.add("gf", _pervec(inp["final_norm_g"]))
    dcw = inp["dn_conv_w"][0]
    P.add("dcw", dcw.reshape(4, 24, 128).transpose(2, 1, 0).reshape(128, 96))
    ccw = inp["cf_conv_w"][0]
    P.add("ccw", ccw.reshape(31, 8, 128).transpose(2, 1, 0).reshape(128, 8 * 31))
    P.add("lng", _pervec(inp["cf_ln_g"][0]))
    P.add("lnb", _pervec(inp["cf_ln_b"][0]))
    fcw = inp["ffn_conv_w"][0]
    nf = FF // 128
    P.add("fcw", fcw.reshape(3, nf, 128).transpose(2, 1, 0).reshape(128, nf * 3))
    P.add("dng", inp["dn_norm_g"][0].reshape(128, 1))
    P.add("alog", inp["dn_a_log"][0].reshape(8, 1))
    P.add("dtb", inp["dn_dt_bias"][0].reshape(8, 1))
    return P


def _prep_weights(inp, FF):
    w_in = inp["w_in"][0]
    o_z, o_b, o_a, o_glu, o_ga, o_gb = 3072, 4096, 4104, 4112, 6160, 8208
    cols = []
    for j in range(8):
        cols += list(range(o_glu + j * 128, o_glu + (j + 1) * 128))
        cols += list(range(o_glu + 1024 + j * 128, o_glu + 1024 + (j + 1) * 128))
    cols += list(range(o_ga, o_ga + 2048))
    cols += list(range(o_gb, o_gb + 2048))
    for h in range(8):
        cols += list(range(h * 128, (h + 1) * 128))
        cols += list(range(1024 + h * 128, 1024 + (h + 1) * 128))
        cols += list(range(2048 + h * 128, 2048 + (h + 1) * 128))
        cols += list(range(o_z + h * 128, o_z + (h + 1) * 128))
    cols = np.asarray(cols)
    W = {}
    W["win"] = _relayout(w_in[:, cols], 256)
    W["wba"] = _relayout(w_in[:, o_b:o_b + 16], 16)
    W["wada"] = _relayout(inp["w_ada"][0], 256)
    W["wab"] = _relayout(np.concatenate([inp["dn_w_o"][0], inp["cf_w_o"][0]], 0), 256)
    W["wout"] = _relayout(inp["w_out"][0], 256)
    wup = inp["ffn_w_up"][0]
    nf = FF // 128
    ucols = []
    for f in range(nf):
        ucols += list(range(f * 128, (f + 1) * 128))
        ucols += list(range(FF + f * 128, FF + (f + 1) * 128))
    W["wup"] = _relayout(wup[:, np.asarray(ucols)], 256)
    W["wdn"] = _relayout(inp["ffn_w_down"][0], 128)
    return W


class _Stop(Exception):
    pass


def build_program(S, FF, dbg=(), stage=99):
    NT = S // 512
    NCH = S // 64
    NF = FF // 128
    nc = bass.Bass("TRN2", target_bir_lowering=False)
    CC = _const_cols(S)
    dummy = {"c": np.zeros((1, D), np.float32), "b_ada": np.zeros((1, 6 * D), np.float32),
             "norm1_g": np.zeros((1, D), np.float32), "norm2_g": np.zeros((1, D), np.float32),
             "final_norm_g": np.zeros((D,), np.float32), "dn_conv_w": np.zeros((1, 4, 3072), np.float32),
             "cf_conv_w": np.zeros((1, 31, 1024), np.float32), "cf_ln_g": np.zeros((1, 1024), np.float32),
             "cf_ln_b": np.zeros((1, 1024), np.float32), "ffn_conv_w": np.zeros((1, 3, FF), np.float32),
             "dn_norm_g": np.zeros((1, 128), np.float32), "dn_a_log": np.zeros((1, 8), np.float32),
             "dn_dt_bias": np.zeros((1, 8), np.float32)}
    PC = _param_cols(dummy, 0, FF)
    NPAR, NCON = PC.n, CC.n

    def din(name, shape):
        return nc.dram_tensor(name, list(shape), F32, kind="ExternalInput").ap()

    xT_d = din("xT", [D, S])
    par_d = din("params", [128, NPAR])
    con_d = din("consts", [128, NCON])
    seg_d = din("seg", [8, S])
    wd = {"win": din("win", [40, 128, 4096]), "wba": din("wba", [1, 128, 256]),
          "wada": din("wada", [48, 128, 4096]), "wab": din("wab", [8, 128, 4096]), "wout": din("wout", [8, 128, 4096]),
          "wup": din("wup", [NF, 128, 4096]), "wdn": din("wdn", [16, 128, NF * 128])}
    out_d = nc.dram_tensor("outT", [D, S], F32, kind="ExternalOutput").ap()

    def scratch(name, shape, dt):
        return nc.dram_tensor(name, list(shape), dt, kind="Internal").ap()

    ga_s = scratch("ga_s", [16, 128, S], BF16)
    gb_s = scratch("gb_s", [16, 128, S], BF16)
    cf_s = scratch("cf_s", [8, 128, S], BF16)
    og_s = scratch("og_s", [8, 128, S], BF16)
    x1_s = scratch("x1_s", [16, 128, S], F32)
    x2_s = scratch("x2_s", [16, 128, S], F32)
    h_s = scratch("h_s", [NF, 128, S], BF16)
    ga_b = [Buf(f"ga_s{i}") for i in range(16)]
    gb_b = [Buf(f"gb_s{i}") for i in range(16)]
    cf_b = [Buf(f"cf_s{i}") for i in range(8)]
    og_b = [Buf(f"og_s{i}") for i in range(8)]
    x1_b = [Buf(f"x1_s{i}") for i in range(16)]
    x2_b = [Buf(f"x2_s{i}") for i in range(16)]
    h_b = [Buf(f"h_s{i}") for i in range(NF)]
    dbg_out = {}
    for nm, shp in dbg:
        dbg_out[nm] = nc.dram_tensor("dbg_" + nm, list(shp), F32, kind="ExternalOutput").ap()

    Sd = Sched()
    st = ExitStack()
    ar = Arena(nc, st, 206 * 1024)
    ps = [st.enter_context(nc.psum_tensor(f"ps{i}", [128, 512], F32)) for i in range(8)]
    psb = [Buf(f"ps{i}", psum=True) for i in range(8)]
    rot = {}

    def bank(pool):
        i = rot.get(id(pool), 0)
        rot[id(pool)] = i + 1
        return pool[i % len(pool)]

    def _fsz(ap):
        n = 1
        for d in ap.shape[1:]:
            n *= d
        return n

    def MM(out, lhsT, rhs, start, stop, r, w):
        c = max(_fsz(out) / 2.4 + 10.0, 56.0)
        if lhsT.dtype == F32:
            c *= 4.0
        Sd.add("pe", lambda e: e.matmul(out, lhsT=lhsT, rhs=rhs, start=start, stop=stop), r, w, cost=c)

    def ACT(out, in_, func, r, w, bias=0.0, scale=1.0):
        Sd.add("act", lambda e: e.activation(out=out, in_=in_, func=func, bias=bias, scale=scale), r, w,
               cost=(_fsz(out) + 300.0) / 1.2)

    _WHATIF = {"scale": 1.0}

    def _vc(out, eng="dve"):
        c = (_fsz(out) + 170.0) / 0.96 * _WHATIF["scale"]
        return c * 2.6 if eng == "pool" else c

    def STT(out, in0, scalar, in1, op0, op1, r, w, eng="dve"):
        Sd.add(eng, lambda e: e.scalar_tensor_tensor(out=out, in0=in0, scalar=scalar, in1=in1, op0=op0, op1=op1), r, w, cost=_vc(out))

    def TT(out, in0, in1, op, r, w, eng="dve"):
        Sd.add(eng, lambda e: e.tensor_tensor(out=out, in0=in0, in1=in1, op=op), r, w, cost=_vc(out, eng))

    def TS(out, in0, s1, s2, op0, op1, r, w, eng="dve"):
        if s2 is None:
            Sd.add(eng, lambda e: e.tensor_scalar(out=out, in0=in0, scalar1=s1, scalar2=None, op0=op0), r, w, cost=_vc(out))
        else:
            Sd.add(eng, lambda e: e.tensor_scalar(out=out, in0=in0, scalar1=s1, scalar2=s2, op0=op0, op1=op1), r, w, cost=_vc(out))

    def CP(out, in_, r, w, eng="dve"):
        Sd.add(eng, lambda e: e.tensor_copy(out=out, in_=in_), r, w, cost=_vc(out))

    import os as _os2
    _RC = float(_os2.environ.get("K_RCOST", "6.5"))

    def RECIP(out, in_, r, w):
        Sd.add("dve", lambda e: e.reciprocal(out=out, in_=in_), r, w, cost=_fsz(out) * _RC + 150.0)

    USE_LNEXP = _os2.environ.get("K_LNEXP", "1") == "1"

    def RSQRT(dst, src, r, w, bias, scale=1.0):
        if USE_LNEXP:
            ACT(dst, src, AF.Ln, r, w, bias=bias, scale=scale)
            ACT(dst, dst, AF.Exp, w, w, scale=-0.5)
        else:
            ACT(dst, src, AF.Sqrt, r, w, bias=bias, scale=scale)
            RECIP(dst, dst, w, w)

    def MEMSET(ap, val, w, eng="dve"):
        Sd.add(eng, lambda e: e.memset(ap, val), (), w, cost=_vc(ap))

    def DBG(name, ap, r):
        if name in dbg_out:
            t = Tile(ar, "dbg_" + name, ap.shape[-1] if len(ap.shape) == 2 else int(np.prod(ap.shape[1:])), F32)
            np_ = ap.shape[0]
            v = t.ap[0:np_, :]
            CP(v, ap, r, [t.b])
            Sd.out_dma.append(Sd.dma("sp", dbg_out[name][0:np_, :], v, r=[t.b]))

    def CKPT(n):
        if stage <= n:
            raise _Stop()

    try:
        par = Tile(ar, "par", NPAR, F32)
        con = Tile(ar, "con", NCON, F32)
        Sd.dma("sp", par.ap, par_d, w=[par.b])
        Sd.dma("sp", con.ap, con_d, w=[con.b])

        def pcol(name, i=0, n=1, rows=128):
            o, _ = PC.off[name]
            return par.ap[0:rows, o + i: o + i + n]

        def ccol(name, i, n, rows=128):
            o, _ = CC.off[name]
            return con.ap[0:rows, o + i: o + i + n]

        identb = Tile(ar, "identb", 128, BF16)
        onesb = Tile(ar, "onesb", 128, BF16)
        CP(identb.ap, ccol("ident", 0, 128), [con.b], [identb.b])
        MEMSET(onesb.ap, 1.0, [onesb.b])
        mod = Tile(ar, "mod", 96, F32, nbufs=2)
        gs1 = Tile(ar, "gs1", 16, F32)
        gs2 = Tile(ar, "gs2", 16, F32)
        gt1q = Tile(ar, "gt1q", 16, F32)
        hw = Tile(ar, "hw", 96 + 248 + NF * 3 + 16, F32)
        o_dcw, o_ccw, o_fcw, o_hg, o_hb = 0, 96, 96 + 248, 96 + 248 + NF * 3, 96 + 248 + NF * 3 + 8
        TS(hw.ap[:, o_dcw:o_dcw + 96], pcol("dcw", 0, 96), 0.5, None, ALU.mult, None, [par.b], [hw.b])
        TS(hw.ap[:, o_ccw:o_ccw + 248], pcol("ccw", 0, 248), 0.5, None, ALU.mult, None, [par.b], [hw.b])
        TS(hw.ap[:, o_fcw:o_fcw + NF * 3], pcol("fcw", 0, NF * 3), 0.5, None, ALU.mult, None, [par.b], [hw.b])
        TS(hw.ap[:, o_hg:o_hg + 8], pcol("lng", 0, 8), 0.5, None, ALU.mult, None, [par.b], [hw.b])
        TS(hw.ap[:, o_hb:o_hb + 8], pcol("lnb", 0, 8), 0.5, None, ALU.mult, None, [par.b], [hw.b])

        NB = 2
        worderA = []
        worderA += [("win", g, 4096) for g in range(40)]
        worderAda1 = [("wada", g, 4096) for g in range(16)]
        worderAda2 = [("wada", g, 4096) for g in range(16, 48)]
        worderA += [("wab", g, 4096) for g in range(8)]
        worderA += [("wout", g, 4096) for g in range(8)]
        worderA += [("wup", f, 4096) for f in range(NF)]
        worderB = []
        for half in range(max(1, S // 1024)):
            worderB += [("wdn", n, NF * 128) for n in range(16)]

        class WStream:
            def __init__(self, order, name, elems, nb=NB):
                self.order = order
                self.nb = nb
                self.bufs = [Tile(ar, f"{name}{i}", elems, BF16) for i in range(nb)]
                self.issued = 0
                self.next = 0

            def get(self, expect):
                i = self.next
                assert self.order[i][0] == expect, (self.order[i], expect)
                while self.issued < min(len(self.order), i + self.nb):
                    nm, g, n = self.order[self.issued]
                    t = self.bufs[self.issued % self.nb]
                    Sd.dma("pool", t.ap[:, 0:n], wd[nm][g], w=[t.b])
                    self.issued += 1
                self.next = i + 1
                return self.bufs[i % self.nb]

            def done(self):
                assert self.next == len(self.order)
                for t in self.bufs:
                    t.free()

        wsA = WStream(worderA, "wbufA", 4096)
        wsAda = [WStream(worderAda1, "wbufAda", 4096)]
        cur_ws = [wsA]

        def next_weight(expect):
            return cur_ws[0].get(expect)

        cact = Tile(ar, "cact", 16, BF16)
        tmp16 = Tile(ar, "tmp16", 16, F32)
        ACT(tmp16.ap, pcol("c", 0, 16), AF.Tanh, [par.b], [tmp16.b], scale=0.5)
        STT(tmp16.ap, tmp16.ap, 1.0, pcol("c", 0, 16), ALU.add, ALU.mult, [tmp16.b, par.b], [tmp16.b])
        TS(cact.ap, tmp16.ap, 0.5, None, ALU.mult, None, [tmp16.b], [cact.b])

        def ada_mm(g, pbank):
            wt = wsAda[0].get("wada")
            for n in range(2):
                chunk = g * 2 + n
                for kc in range(KC):
                    MM(ps[pbank][:, chunk:chunk + 1], wt.ap[:, kc * 256 + n * 128: kc * 256 + n * 128 + 128],
                       cact.ap[:, kc:kc + 1], kc == 0, kc == KC - 1, [wt.b, cact.b], [psb[pbank]])

        def ada_fin(g0, g1, pbank, mbuf):
            c0, c1 = g0 * 2, g1 * 2
            TT(mod.ap[:, c0:c1], ps[pbank][:, c0:c1], pcol("b_ada", c0, c1 - c0), ALU.add, [psb[pbank], par.b], [mbuf])

        def ada_groups(g0, g1, pbank, mbuf):
            for g in range(g0, g1):
                ada_mm(g, pbank)
            ada_fin(g0, g1, pbank, mbuf)


        xl = []
        xl_i = [0]
        gen = [0]

        def alloc_xl():
            gen[0] += 1
            xl[:] = [Tile(ar, f"xl{gen[0]}_{i}", S, F32) for i in range(2)]

        def free_xl():
            for t in xl:
                t.free()
            xl[:] = []

        alloc_xl()

        def load_chunk(src_ap, srcbufs):
            t = xl[xl_i[0] % 2]
            xl_i[0] += 1
            Sd.dma("sp", t.ap, src_ap, r=srcbufs, w=[t.b])
            return t

        sqb = [Tile(ar, f"sqb{i}", 512, BF16) for i in range(3)]
        sq_i = [0]

        deferred = []

        def flush_deferred(keep=0):
            while len(deferred) > keep:
                deferred.pop(0)()

        def sumsq_accum(src_ap, srcbufs, tt, first, last, pbank, defer=False):
            t = sqb[sq_i[0] % len(sqb)]
            sq_i[0] += 1
            ACT(t.ap, src_ap, AF.Square, srcbufs, [t.b])
            fn = lambda: MM(ps[pbank][:, :], onesb.ap, t.ap, first, last, [onesb.b, t.b], [psb[pbank]])
            if defer:
                deferred.append(fn)
            else:
                fn()

        def rstd_from(pbank, dst_ap, dstbufs, inv_n, scale_extra=1.0):
            RSQRT(dst_ap, ps[pbank][:, :], [psb[pbank]], dstbufs, epsb.ap[:, 0:1], inv_n)

        epsb = Tile(ar, "epsb", 4, F32)
        MEMSET(epsb.ap[:, 0:1], EPS, [epsb.b])
        MEMSET(epsb.ap[:, 1:2], 1.0, [epsb.b])
        MEMSET(epsb.ap[:, 2:3], 128.0 * EPS * EPS, [epsb.b])

        hn = [Tile(ar, f"hn{k}", S, BF16) for k in range(KC)]
        rstd = Tile(ar, "rstd", S, F32)

        def norm_modulate(src_of, srcbufs_of, gs, sh_c0, modbuf):
            gen[0] += 1
            ntmp = [Tile(ar, f"ntmp{gen[0]}_{i}", S, F32) for i in range(2)]
            for k in range(KC):
                xt = load_chunk(src_of(k), srcbufs_of(k))
                tm = ntmp[k % 2]
                STT(tm.ap, xt.ap, gs.ap[:, k:k + 1], rstd.ap, ALU.mult, ALU.mult, [xt.b, gs.b, rstd.b], [tm.b])
                ACT(hn[k].ap, tm.ap, AF.Identity, [tm.b, modbuf], [hn[k].b], bias=mod.ap[:, sh_c0 + k: sh_c0 + k + 1])
            for t in ntmp:
                t.free()

        SB = [4, 5, 6, 7]
        for k in range(KC):
            xt = load_chunk(xT_d[k * 128:(k + 1) * 128, :], [])
            for tt in range(NT):
                sumsq_accum(xt.ap[:, tt * 512:(tt + 1) * 512], [xt.b], tt, k == 0, k == KC - 1, SB[tt])
        for tt in range(NT):
            rstd_from(SB[tt], rstd.ap[:, tt * 512:(tt + 1) * 512], [rstd.b], 1.0 / D)
        ada_groups(0, 16, 0, mod.bufs[0])
        STT(gs1.ap, mod.ap[:, 16:32], 1.0, pcol("g1", 0, 16), ALU.add, ALU.mult, [mod.bufs[0], par.b], [gs1.b])
        wsAda[0].done()
        norm_modulate(lambda k: xT_d[k * 128:(k + 1) * 128, :], lambda k: [], gs1, 0, mod.bufs[0])
        DBG("hn0", hn[0].ap[:, 0:512], [hn[0].b])
        CKPT(1)
        free_xl()
        rstd.free()

        hn_bufs = [t.b for t in hn]

        def proj(wt, col0, gw, tt, pbank, rows=128, ncols=128):
            for kc in range(KC):
                MM(ps[pbank][0:ncols, :], wt.ap[:, kc * gw + col0: kc * gw + col0 + ncols],
                   hn[kc].ap[:, tt * 512:(tt + 1) * 512], kc == 0, kc == KC - 1, [wt.b, hn[kc].b], [psb[pbank]])

        MMB = [0, 1, 2, 3]
        AUX = [4, 5, 6, 7]

        wba = Tile(ar, "wba", 256, BF16)
        Sd.dma("pool", wba.ap, wd["wba"][0], w=[wba.b])
        rows = Tile(ar, "rows", 2 * S, F32)
        rowsB = Tile(ar, "rowsB", 2 * S, F32)
        R_BETA, R_GC = [rows.ap[0:8, i * S:(i + 1) * S] for i in range(2)]
        R_T1, R_T2 = [rowsB.ap[0:8, i * S:(i + 1) * S] for i in range(2)]
        R_G = R_T1
        R_DL = R_T1
        nA = Tile(ar, "nA", 2, F32)
        ACT(nA.ap[0:8, 0:1], pcol("alog", 0, 1, rows=8), AF.Exp, [par.b], [nA.b])
        TS(nA.ap[0:8, 0:1], nA.ap[0:8, 0:1], -1.0, None, ALU.mult, None, [nA.b], [nA.b])
        for tt in range(NT):
            sl = slice(tt * 512, (tt + 1) * 512)
            pb = bank(MMB)
            for kc in range(KC):
                MM(ps[pb][0:8, :], wba.ap[:, kc * 16: kc * 16 + 8], hn[kc].ap[:, sl], kc == 0, kc == KC - 1,
                   [wba.b, hn[kc].b], [psb[pb]])
            ACT(R_T2[:, sl], ps[pb][0:8, :], AF.Tanh, [psb[pb]], [rowsB.b], scale=0.5)
            TS(R_BETA[:, sl], R_T2[:, sl], 0.5, 0.5, ALU.mult, ALU.add, [rowsB.b], [rows.b])
            pb2 = bank(MMB)
            for kc in range(KC):
                MM(ps[pb2][0:8, :], wba.ap[:, kc * 16 + 8: kc * 16 + 16], hn[kc].ap[:, sl], kc == 0, kc == KC - 1,
                   [wba.b, hn[kc].b], [psb[pb2]])
            ACT(R_T2[:, sl], ps[pb2][0:8, :], AF.Exp, [psb[pb2], par.b], [rowsB.b], bias=pcol("dtb", 0, 1, rows=8))
            ACT(R_T2[:, sl], R_T2[:, sl], AF.Ln, [rowsB.b], [rowsB.b], bias=epsb.ap[0:8, 1:2])
            TS(R_G[:, sl], R_T2[:, sl], nA.ap[0:8, 0:1], None, ALU.mult, None, [rowsB.b, nA.b], [rowsB.b])
        Sd.dma("sp", R_T2, seg_d, w=[rowsB.b])
        Sd.add("dve", lambda e: e.tensor_tensor_scan(out=R_GC, data0=R_T2, data1=R_G, initial=0.0,
                                                      op0=ALU.mult, op1=ALU.add), [rowsB.b], [rows.b])
        tot = Tile(ar, "tot", NCH, F32)
        CP(tot.ap[0:8, :], R_GC.rearrange("p (n c) -> p n c", c=64)[:, :, 63], [rows.b], [tot.b])
        TT(R_DL.rearrange("p (n c) -> p n c", c=64), tot.ap[0:8, :].unsqueeze(2).to_broadcast([8, NCH, 64]),
           R_GC.rearrange("p (n c) -> p n c", c=64), ALU.subtract, [tot.b, rows.b], [rowsB.b])
        tabs = Tile(ar, "tabs", 4 * NCH * 8, F32)
        NC8 = NCH * 8
        T_GC, T_BETA, T_BEG, T_EDL = [tabs.ap[0:64, i * NC8:(i + 1) * NC8] for i in range(4)]
        for (src, dst) in ((R_GC, T_GC), (R_BETA, T_BETA), (R_DL, T_EDL)):
            for c0 in range(0, NCH, 64):
                pb = bank(AUX)
                nn = min(64, NCH - c0)
                for n in range(nn):
                    MM(ps[pb][0:64, n * 8:(n + 1) * 8], src[:, (c0 + n) * 64:(c0 + n + 1) * 64], ccol("ident", 0, 8, rows=8),
                       True, True, [rows.b, rowsB.b, con.b], [psb[pb]])
                CP(dst[:, c0 * 8:(c0 + nn) * 8], ps[pb][0:64, 0:nn * 8], [psb[pb]], [tabs.b])
        ACT(T_EDL, T_EDL, AF.Exp, [tabs.b], [tabs.b])
        ACT(T_BEG, T_GC, AF.Exp, [tabs.b], [tabs.b])
        TT(T_BEG, T_BEG, T_BETA, ALU.mult, [tabs.b], [tabs.b])
        egl = Tile(ar, "egl", 8 * NCH, F32)
        for h in range(H):
            pb = bank(AUX)
            MM(ps[pb][:, 0:NCH], ccol("sel128", h * 128, 128, rows=8), tot.ap[0:8, :], True, True, [con.b, tot.b], [psb[pb]])
            ACT(egl.ap[:, h * NCH:(h + 1) * NCH], ps[pb][:, 0:NCH], AF.Exp, [psb[pb]], [egl.b])
        DBG("rows", rows.ap[0:8, 0:2 * S], [rows.b])
        rowsB.free()
        CKPT(2)
        DBG("tabs", tabs.ap[0:64, :], [tabs.b])

        upad = [Tile(ar, f"upad{i}", 32 + S, BF16) for i in range(2)]
        for t in upad:
            MEMSET(t.ap[:, 0:32], 0.0, [t.b])
        diag = [Tile(ar, f"diag{i}", 31 * 128, BF16) for i in range(2)]
        uc = [Tile(ar, f"uc{j}", S, BF16) for j in range(8)]
        csum = Tile(ar, "csum", S, F32)
        csq = Tile(ar, "csq", S, F32)
        tg = [Tile(ar, f"tg{i}", 512, F32) for i in range(2)]
        tg_i = [0]
        for j in range(8):
            wt = next_weight("win")
            up = upad[j % 2]
            dg = diag[j % 2]
            TT(dg.ap.rearrange("p (k c) -> p k c", c=128),
               identb.ap.unsqueeze(1).to_broadcast([128, 31, 128]),
               hw.ap[:, o_ccw + j * 31: o_ccw + (j + 1) * 31].unsqueeze(2).to_broadcast([128, 31, 128]),
               ALU.mult, [identb.b, hw.b], [dg.b])
            for tt in range(NT):
                pv = bank(MMB)
                proj(wt, 0, 256, tt, pv)
                pg = bank(MMB)
                proj(wt, 128, 256, tt, pg)
                t_ = tg[tg_i[0] % 2]
                tg_i[0] += 1
                ACT(t_.ap, ps[pg][:, :], AF.Tanh, [psb[pg]], [t_.b], scale=0.5)
                STT(up.ap[:, 32 + tt * 512: 32 + (tt + 1) * 512], t_.ap, 1.0, ps[pv][:, :], ALU.add, ALU.mult,
                    [t_.b, psb[pv]], [up.b])
            for tt in range(NT):
                pc = bank(AUX)
                for k in range(31):
                    MM(ps[pc][:, :], dg.ap[:, k * 128:(k + 1) * 128], up.ap[:, 2 + k + tt * 512: 2 + k + tt * 512 + 512],
                       k == 0, k == 30, [dg.b, up.b], [psb[pc]])
                sl = slice(tt * 512, (tt + 1) * 512)
                CP(uc[j].ap[:, sl], ps[pc][:, :], [psb[pc]], [uc[j].b], eng="act" if False else "dve")
                sq_t = sqb[sq_i[0] % 2]
                sq_i[0] += 1
                ACT(sq_t.ap, ps[pc][:, :], AF.Square, [psb[pc]], [sq_t.b])
                p1 = bank(AUX)
                MM(ps[p1][:, :], onesb.ap, uc[j].ap[:, sl], True, True, [onesb.b, uc[j].b], [psb[p1]])
                if j == 0:
                    CP(csum.ap[:, sl], ps[p1][:, :], [psb[p1]], [csum.b])
                else:
                    TT(csum.ap[:, sl], csum.ap[:, sl], ps[p1][:, :], ALU.add, [csum.b, psb[p1]], [csum.b])
                p2 = bank(AUX)
                MM(ps[p2][:, :], onesb.ap, sq_t.ap, True, True, [onesb.b, sq_t.b], [psb[p2]])
                if j == 0:
                    CP(csq.ap[:, sl], ps[p2][:, :], [psb[p2]], [csq.b])
                else:
                    TT(csq.ap[:, sl], csq.ap[:, sl], ps[p2][:, :], ALU.add, [csq.b, psb[p2]], [csq.b])
        for t in upad + diag:
            t.free()
        lnt = [Tile(ar, f"lnt{i}", 512, F32) for i in range(2)]
        TS(csum.ap, csum.ap, 1.0 / 1024, None, ALU.mult, None, [csum.b], [csum.b])
        TS(csq.ap, csq.ap, 1.0 / 1024, None, ALU.mult, None, [csq.b], [csq.b])
        for tt in range(NT):
            sl = slice(tt * 512, (tt + 1) * 512)
            m2 = lnt[0]
            TT(m2.ap, csum.ap[:, sl], csum.ap[:, sl], ALU.mult, [csum.b], [m2.b])
            TT(csq.ap[:, sl], csq.ap[:, sl], m2.ap, ALU.subtract, [csq.b, m2.b], [csq.b])
        RSQRT(csq.ap, csq.ap, [csq.b], [csq.b], epsb.ap[:, 0:1])
        cfa = [Tile(ar, f"cfa{i}", S, BF16) for i in range(2)]
        for j in range(8):
            o_ = cfa[j % 2]
            for tt in range(NT):
                sl = slice(tt * 512, (tt + 1) * 512)
                d1 = lnt[0]
                d2 = lnt[1]
                TT(d1.ap, uc[j].ap[:, sl], csum.ap[:, sl], ALU.subtract, [uc[j].b, csum.b], [d1.b])
                TT(d1.ap, d1.ap, csq.ap[:, sl], ALU.mult, [d1.b, csq.b], [d1.b])
                ACT(d2.ap, d1.ap, AF.Tanh, [d1.b, hw.b], [d2.b], bias=hw.ap[:, o_hb + j: o_hb + j + 1], scale=hw.ap[:, o_hg + j: o_hg + j + 1])
                ACT(d1.ap, d1.ap, AF.Identity, [d1.b, par.b], [d1.b], bias=pcol("lnb", j, 1), scale=pcol("lng", j, 1))
                STT(o_.ap[:, sl], d2.ap, 1.0, d1.ap, ALU.add, ALU.mult, [d2.b, d1.b], [o_.b])
            Sd.dma("sp", cf_s[j], o_.ap, r=[o_.b], w=[cf_b[j]])
            if j == 0:
                DBG("cfa0", o_.ap[:, 0:512], [o_.b])
        for t in uc + [csum, csq] + cfa + lnt + tg:
            t.free()

        CKPT(3)

        wsAda[0] = WStream(worderAda2, "wbufAdb", 4096)
        gst = [Tile(ar, f"gst{i}", S, BF16) for i in range(2)]
        gi = 0
        for which, dst, dbufs in (("a", ga_s, ga_b), ("b", gb_s, gb_b)):
            for g in range(8):
                gidx = (0 if which == "a" else 8) + g
                wt = next_weight("win")
                for n in range(2):
                    t = gst[gi % 2]
                    gi += 1
                    for tt in range(NT):
                        pb = bank(MMB)
                        proj(wt, n * 128, 256, tt, pb)
                        ACT(t.ap[:, tt * 512:(tt + 1) * 512], ps[pb][:, :], AF.Tanh, [psb[pb]], [t.b], scale=0.5)
                    Sd.dma("sp", dst[g * 2 + n], t.ap, r=[t.b], w=[dbufs[g * 2 + n]])
                ada_mm(16 + 2 * gidx, 7)
                ada_mm(17 + 2 * gidx, 7)
        for t in gst:
            t.free()
        ada_fin(16, 48, 7, mod.bufs[1])
        wsAda[0].done()
        STT(gs2.ap, mod.ap[:, 64:80], 1.0, pcol("g2", 0, 16), ALU.add, ALU.mult, [mod.bufs[1], par.b], [gs2.b])
        TS(gt1q.ap, mod.ap[:, 32:48], 0.25, None, ALU.mult, None, [mod.bufs[1]], [gt1q.b])

        CKPT(4)
        raw = [Tile(ar, f"raw{i}", 4 + 512, F32) for i in range(2)]
        acc = [Tile(ar, f"acc{i}", 512, F32) for i in range(2)]
        tnt = [Tile(ar, f"tnt{i}", 512, F32) for i in range(2)]
        rsq = Tile(ar, "rsq", 512, F32)
        sqq = Tile(ar, "sqq", S, BF16)
        sqql = [Tile(ar, f"sqql{i}", 512, BF16) for i in range(2)]
        qkv = [Tile(ar, f"qkv{i}", S, BF16) for i in range(3)]
        zs2 = [Tile(ar, "zs", S, BF16)] * 2
        ogt = Tile(ar, "ogt", S, BF16)
        S32 = Tile(ar, "S32", 128, F32)
        Sbp = [Tile(ar, f"Sb{i}", 128, BF16) for i in range(2)]
        kbg = Tile(ar, "kbg", 1024, BF16)
        kdec = [Tile(ar, "kdec", 1024, BF16)] * 2
        wtm = Tile(ar, "wtm", 1024, BF16)
        vtm = Tile(ar, "vtm", 1024, BF16)
        coef = Tile(ar, "coef", 4 * 512, F32)
        Amat = Tile(ar, "Amat", 4 * 512, BF16)
        Xm = Tile(ar, "Xm", 512, F32)
        Xb = Tile(ar, "Xb", 512, BF16)
        attnTp = [Tile(ar, f"attnT{i}", 512, BF16) for i in range(2)]
        u0p = [Tile(ar, f"u0b{i}", 1024, BF16) for i in range(2)]
        KTp = [Tile(ar, f"KT{i}", 1024, BF16) for i in range(2)]
        Hnp = [Tile(ar, f"Hn{i}", 1024, BF16) for i in range(2)]
        qdecp = [Tile(ar, f"QpT{i}", 512, BF16) for i in range(2)]
        otile = Tile(ar, "otile", 512, F32)
        orstd = Tile(ar, "orstd", 512, F32)
        import os as _os
        POOLE = _os.environ.get("K_POOL", "dve")
        TEV = _os.environ.get("K_TEV", "dve")
        P1B = [int(c) for c in _os.environ.get("K_P1B", "0123")]
        SCB = [int(c) for c in _os.environ.get("K_SCB", "6")]
        PJB = [int(c) for c in _os.environ.get("K_PJB", "2345")]
        OB = 7
        ai_ = [0]
        ACTCP = lambda o, i, r, w: Sd.add("act", lambda e: e.activation(out=o, in_=i, func=AF.Copy), r, w, cost=(_fsz(o) + 300.0) / 1.2)

        acc3 = acc + [rsq]

        def gen_proj(h):
            zs = zs2[h % 2]
            items = [dict(ci=ci, tt=tt, i=ci * NT + tt) for ci in range(4) for tt in range(NT)]
            wts = {}

            def S1(it):
                ci, tt = it["ci"], it["tt"]
                if ci % 2 == 0 and tt == 0:
                    wts["cur"] = next_weight("win")
                it["pb"] = bank(PJB)
                proj(wts["cur"], (ci % 2) * 128, 256, tt, it["pb"])

            def S2(it):
                ci, tt, pb = it["ci"], it["tt"], it["pb"]
                sl = slice(tt * 512, (tt + 1) * 512)
                if ci == 3:
                    t_ = tnt[it["i"] % 2]
                    ACT(t_.ap, ps[pb][:, :], AF.Tanh, [psb[pb]], [t_.b], scale=0.5)
                    STT(zs.ap[:, sl], t_.ap, 1.0, ps[pb][:, :], ALU.add, ALU.mult, [t_.b, psb[pb]], [zs.b])
                    return
                wofs = o_dcw + (ci * 8 + h) * 4
                rw = raw[tt % 2]
                if tt == 0:
                    MEMSET(rw.ap[:, 0:4], 0.0, [rw.b])
                else:
                    pv_ = raw[(tt - 1) % 2]
                    ACTCP(rw.ap[:, 1:4], pv_.ap[:, 513:516], [pv_.b], [rw.b])
                ACTCP(rw.ap[:, 4:516], ps[pb][:, :], [psb[pb]], [rw.b])
                ac = acc3[it["i"] % 3]
                tn = tnt[it["i"] % 2]
                it["ac"] = ac
                if _os2.environ.get("K_CONV0"):
                    _WHATIF["scale"] = 0.25
                TS(ac.ap, rw.ap[:, 1:513], hw.ap[:, wofs:wofs + 1], None, ALU.mult, None, [rw.b, hw.b], [ac.b])
                for k in range(1, 4):
                    STT(ac.ap, rw.ap[:, 1 + k:513 + k], hw.ap[:, wofs + k:wofs + k + 1], ac.ap, ALU.mult, ALU.add,
                        [rw.b, hw.b, ac.b], [ac.b])
                _WHATIF["scale"] = 1.0
                ACT(tn.ap, ac.ap, AF.Tanh, [ac.b], [tn.b])
                if ci == 2:
                    STT(qkv[2].ap[:, sl], tn.ap, 1.0, ac.ap, ALU.add, ALU.mult, [tn.b, ac.b], [qkv[2].b])
                else:
                    STT(ac.ap, tn.ap, 1.0, ac.ap, ALU.add, ALU.mult, [tn.b, ac.b], [ac.b])
                    if ci == 0:
                        ACTCP(qkv[0].ap[:, sl], ac.ap, [ac.b], [qkv[0].b])

            def S3(it):
                ci, tt = it["ci"], it["tt"]
                if ci > 1:
                    return
                sl = slice(tt * 512, (tt + 1) * 512)
                ac = it["ac"]
                pq = bank(PJB)
                sumsq_accum(ac.ap, [ac.b], tt, True, True, pq)
                if ci == 0:
                    if tt == NT - 1:
                        ACTCP(sqql[h % 2].ap, ps[pq][:, :], [psb[pq]], [sqql[h % 2].b])
                    else:
                        ACTCP(sqq.ap[:, sl], ps[pq][:, :], [psb[pq]], [sqq.b])
                else:
                    RSQRT(ps[pq][:, :], ps[pq][:, :], [psb[pq]], [psb[pq]], epsb.ap[:, 0:1])
                    TT(qkv[1].ap[:, sl], ac.ap, ps[pq][:, :], ALU.mult, [ac.b, psb[pq]], [qkv[1].b])

            n_it = len(items)
            for t in range(n_it + 2):
                if t < n_it:
                    S1(items[t])
                if 0 <= t - 1 < n_it:
                    if items[t - 1]["ci"] == 3 and items[t - 1]["tt"] == 0:
                        yield "BARRIER"
                    S2(items[t - 1])
                if 0 <= t - 2 < n_it:
                    S3(items[t - 2])
                yield
            if h == 0:
                DBG("q0", qkv[0].ap[:, 0:512], [qkv[0].b])
                DBG("k0", qkv[1].ap[:, 0:512], [qkv[1].b])
                DBG("v0", qkv[2].ap[:, 0:512], [qkv[2].b])

        def gen_p1(h, tt):
            qT, kT, vT = qkv
            par_ = tt % 2
            attnT, u0, qdec, kd = attnTp[par_], u0p[par_], qdecp[par_], kdec[par_]
            KTb, Hnb = KTp[par_], Hnp[par_]
            tsl = slice(tt * 512, (tt + 1) * 512)
            n0 = tt * 8

            def tcol(T, c0, cn):
                base = (n0 + c0) * 8 + h
                return T[:, base: base + (cn - 1) * 8 + 1: 8]

            for half in range(2):
                pk = bank(P1B)
                pv = bank(P1B)
                for c in range(4):
                    cs = slice(tt * 512 + (half * 4 + c) * 64, tt * 512 + (half * 4 + c + 1) * 64)
                    MM(ps[pk][0:64, c * 128:(c + 1) * 128], kT.ap[:, cs], identb.ap, True, True, [kT.b, identb.b], [psb[pk]])
                    MM(ps[pv][0:64, c * 128:(c + 1) * 128], vT.ap[:, cs], identb.ap, True, True, [vT.b, identb.b], [psb[pv]])
                k3 = ps[pk][0:64, :].rearrange("p (c d) -> p c d", d=128)
                v3 = ps[pv][0:64, :].rearrange("p (c d) -> p c d", d=128)
                o1 = kbg.ap[0:64, half * 512:(half + 1) * 512].rearrange("p (c d) -> p c d", d=128)
                o2 = kd.ap[0:64, half * 512:(half + 1) * 512].rearrange("p (c d) -> p c d", d=128)
                o3 = vtm.ap[0:64, half * 512:(half + 1) * 512].rearrange("p (c d) -> p c d", d=128)
                TT(o1, k3, tcol(T_BEG, half * 4, 4).unsqueeze(2).to_broadcast([64, 4, 128]), ALU.mult, [psb[pk], tabs.b], [kbg.b])
                if TEV == "act":
                    for c in range(4):
                        ACT(o2[:, c, :], k3[:, c, :], AF.Identity, [psb[pk], tabs.b], [kd.b], scale=tcol(T_EDL, half * 4 + c, 1))
                    for c in range(4):
                        ACT(o3[:, c, :], v3[:, c, :], AF.Identity, [psb[pv], tabs.b], [vtm.b], scale=tcol(T_BETA, half * 4 + c, 1))
                else:
                    TT(o2, k3, tcol(T_EDL, half * 4, 4).unsqueeze(2).to_broadcast([64, 4, 128]), ALU.mult, [psb[pk], tabs.b], [kd.b])
                    TT(o3, v3, tcol(T_BETA, half * 4, 4).unsqueeze(2).to_broadcast([64, 4, 128]), ALU.mult, [psb[pv], tabs.b], [vtm.b])
                yield
            pkk = bank(P1B)
            pqk = bank(P1B)
            for c in range(8):
                cs = slice(tt * 512 + c * 64, tt * 512 + (c + 1) * 64)
                MM(ps[pkk][0:64, c * 64:(c + 1) * 64], kT.ap[:, cs], kT.ap[:, cs], True, True, [kT.b], [psb[pkk]])
                MM(ps[pqk][0:64, c * 64:(c + 1) * 64], kT.ap[:, cs], qT.ap[:, cs], True, True, [kT.b, qT.b], [psb[pqk]])
            yield
            pg = bank(P1B)
            MM(ps[pg][0:64, :], ccol("sel64", h * 64, 64, rows=8), R_GC[:, tsl], True, True, [con.b, rows.b], [psb[pg]])
            cZ, cAt, cA, cAT = [coef.ap[0:64, i * 512:(i + 1) * 512] for i in range(4)]
            cT = cAt
            r3 = lambda a_: a_.rearrange("p (c y) -> p c y", y=64)
            gcx = tcol(T_GC, 0, 8).unsqueeze(2).to_broadcast([64, 8, 64])
            btx = tcol(T_BETA, 0, 8).unsqueeze(2).to_broadcast([64, 8, 64])
            TT(r3(cZ), r3(ps[pg][0:64, :]), gcx, ALU.subtract, [psb[pg], tabs.b], [coef.b])
            mk = lambda nm: ccol(nm, 0, 64, rows=64).unsqueeze(1).to_broadcast([64, 8, 64])
            TT(r3(cZ), r3(cZ), mk("sgn"), ALU.mult, [coef.b, con.b], [coef.b])
            ACT(cZ, cZ, AF.Exp, [coef.b], [coef.b])
            yield
            pbr = bank(P1B)
            MM(ps[pbr][0:64, :], ccol("sel64", h * 64, 64, rows=8), R_BETA[:, tsl], True, True, [con.b, rows.b], [psb[pbr]])
            TT(r3(cAt), r3(cZ), mk("mge"), ALU.mult, [coef.b, con.b], [coef.b])
            TT(attnT.ap[0:64, :], ps[pqk][0:64, :], cAt, ALU.mult, [psb[pqk], coef.b], [attnT.b])
            TT(r3(cT), r3(cZ), mk("mlt"), ALU.mult, [coef.b, con.b], [coef.b], eng=POOLE)
            TT(r3(cA), r3(cT), btx, ALU.mult, [coef.b, tabs.b], [coef.b], eng=POOLE)
            yield
            TT(r3(cT), r3(cZ), mk("mgt"), ALU.mult, [coef.b, con.b], [coef.b], eng=POOLE)
            TT(cAT, cT, ps[pbr][0:64, :], ALU.mult, [coef.b, psb[pbr]], [coef.b])
            M = [Amat.ap[0:64, 0:512], Amat.ap[0:64, 512:1024]]
            N = [Amat.ap[0:64, 1024:1536], Amat.ap[0:64, 1536:2048]]
            TT(M[0], ps[pkk][0:64, :], cA, ALU.mult, [psb[pkk], coef.b], [Amat.b])
            TT(cT, ps[pkk][0:64, :], cAT, ALU.mult, [psb[pkk], coef.b], [coef.b])
            yield
            ACTCP(N[0], cT, [coef.b], [Amat.b])
            identbc = ccol("ident", 0, 64, rows=64).unsqueeze(1).to_broadcast([64, 8, 64])
            TT(r3(Xb.ap[0:64, :]), identbc, r3(cT), ALU.subtract, [con.b, coef.b], [Xb.b])
            TT(r3(Xm.ap[0:64, :]), identbc, r3(cT), ALU.subtract, [con.b, coef.b], [Xm.b], eng=POOLE)
            yield
            cur = 0
            for lvl in range(1, 6):
                nxt = 1 - cur
                pm = bank(P1B)
                for c in range(8):
                    cs = slice(c * 64, (c + 1) * 64)
                    MM(ps[pm][0:64, cs], N[cur][:, cs], M[cur][:, cs], True, True, [Amat.b], [psb[pm]])
                if lvl < 5:
                    pn = bank(P1B)
                    for c in range(8):
                        cs = slice(c * 64, (c + 1) * 64)
                        MM(ps[pn][0:64, cs], M[cur][:, cs], N[cur][:, cs], True, True, [Amat.b], [psb[pn]])
                ACTCP(M[nxt], ps[pm][0:64, :], [psb[pm]], [Amat.b])
                if lvl < 5:
                    ACTCP(N[nxt], ps[pn][0:64, :], [psb[pn]], [Amat.b])
                yield
                px = bank(P1B)
                for c in range(8):
                    cs = slice(c * 64, (c + 1) * 64)
                    MM(ps[px][0:64, cs], M[nxt][:, cs], Xb.ap[0:64, cs], True, True, [Amat.b, Xb.b], [psb[px]])
                TT(Xb.ap[0:64, :], Xm.ap[0:64, :], ps[px][0:64, :], ALU.add, [Xm.b, psb[px]], [Xb.b])
                if lvl < 5:
                    TT(Xm.ap[0:64, :], Xm.ap[0:64, :], ps[px][0:64, :], ALU.add, [Xm.b, psb[px]], [Xm.b])
                cur = nxt
                yield
            for half in range(2):
                pu = bank(P1B)
                pw = bank(P1B)
                for c in range(4):
                    cc = half * 4 + c
                    MM(ps[pu][0:64, c * 128:(c + 1) * 128], Xb.ap[0:64, cc * 64:(cc + 1) * 64],
                       vtm.ap[0:64, cc * 128:(cc + 1) * 128], True, True, [Xb.b, vtm.b], [psb[pu]])
                    MM(ps[pw][0:64, c * 128:(c + 1) * 128], Xb.ap[0:64, cc * 64:(cc + 1) * 64],
                       kbg.ap[0:64, cc * 128:(cc + 1) * 128], True, True, [Xb.b, kbg.b], [psb[pw]])
                ACTCP(u0.ap[0:64, half * 512:(half + 1) * 512], ps[pu][0:64, :], [psb[pu]], [u0.b])
                CP(wtm.ap[0:64, half * 512:(half + 1) * 512], ps[pw][0:64, :], [psb[pw]], [wtm.b])
                yield
            for half in range(2):
                pk_ = bank(P1B)
                ph_ = bank(P1B)
                for c in range(4):
                    cc = half * 4 + c
                    c128 = slice(cc * 128, (cc + 1) * 128)
                    MM(ps[pk_][:, c * 128:(c + 1) * 128], wtm.ap[0:64, c128], kd.ap[0:64, c128], True, True,
                       [wtm.b, kd.b], [psb[pk_]])
                    MM(ps[ph_][:, c * 128:(c + 1) * 128], kd.ap[0:64, c128], u0.ap[0:64, c128], True, True,
                       [kd.b, u0.b], [psb[ph_]])
                ACTCP(KTb.ap[:, half * 512:(half + 1) * 512], ps[pk_][:, :], [psb[pk_]], [KTb.b])
                ACT(Hnb.ap[:, half * 512:(half + 1) * 512], ps[ph_][:, :], AF.Identity, [psb[ph_]], [Hnb.b], scale=-1.0)
                yield
            p8 = bank(P1B)
            MM(ps[p8][:, :], ccol("sel128", h * 128, 128, rows=8), R_GC[:, tsl], True, True, [con.b, rows.b], [psb[p8]])
            ACT(ps[p8][:, :], ps[p8][:, :], AF.Exp, [psb[p8]], [psb[p8]])
            pq_ = bank(P1B)
            for c in range(8):
                MM(ps[pq_][:, c * 64:(c + 1) * 64], wtm.ap[0:64, c * 128:(c + 1) * 128], attnT.ap[0:64, c * 64:(c + 1) * 64],
                   True, True, [wtm.b, attnT.b], [psb[pq_]])
            qtmp = coef.ap[:, 0:512]
            TT(qtmp, qT.ap[:, tsl], ps[p8][:, :], ALU.mult, [qT.b, psb[p8]], [coef.b])
            TT(qdec.ap, qtmp, ps[pq_][:, :], ALU.subtract, [coef.b, psb[pq_]], [qdec.b])
            if h == 0 and tt == 0:
                DBG("attnT", attnT.ap[0:64, :], [attnT.b])
                DBG("Xm", Xm.ap[0:64, :], [Xm.b])
            yield

        def gen_scan(h, tt):
            par_ = tt % 2
            attnT, u0, qdec = attnTp[par_], u0p[par_], qdecp[par_]
            KTb, Hnb = KTp[par_], Hnp[par_]
            zs = zs2[h % 2]
            tsl = slice(tt * 512, (tt + 1) * 512)
            n0 = tt * 8
            for c in range(8):
                n = n0 + c
                Sc, Sn = Sbp[n % 2], Sbp[(n + 1) % 2]
                cs64 = slice(c * 64, (c + 1) * 64)
                c128 = slice(c * 128, (c + 1) * 128)
                pk = bank(SCB)
                MM(ps[pk][:, 0:128], KTb.ap[:, c128], Sc.ap, True, False, [KTb.b, Sc.b], [psb[pk]])
                MM(ps[pk][:, 0:128], identb.ap, Hnb.ap[:, c128], False, True, [identb.b, Hnb.b], [psb[pk]])
                MM(ps[OB][:, cs64], Sc.ap, qdec.ap[:, cs64], True, False, [Sc.b, qdec.b], [psb[OB]])
                MM(ps[OB][:, cs64], u0.ap[0:64, c128], attnT.ap[0:64, cs64], False, True, [u0.b, attnT.b], [psb[OB]])
                yield
                eg = egl.ap[:, h * NCH + n: h * NCH + n + 1]
                STT(Sn.ap, S32.ap, eg, ps[pk][:, 0:128], ALU.mult, ALU.subtract, [S32.b, egl.b, psb[pk]], [Sn.b])
                STT(S32.ap, S32.ap, eg, ps[pk][:, 0:128], ALU.mult, ALU.subtract, [S32.b, egl.b, psb[pk]], [S32.b])
                yield
            ACTCP(otile.ap, ps[OB][:, :], [psb[OB]], [otile.b])
            pb = bank(SCB)
            sumsq_accum(ps[OB][:, :], [psb[OB]], tt, True, True, pb)
            yield
            sq_t, sq_ap = (sqql[h % 2], sqql[h % 2].ap) if tt == NT - 1 else (sqq, sqq.ap[:, tsl])
            STT(orstd.ap, sq_ap, 128.0 * 128.0 * EPS, ps[pb][:, :], ALU.mult, ALU.add, [sq_t.b, psb[pb]], [orstd.b])
            RSQRT(orstd.ap, orstd.ap, [orstd.b], [orstd.b], epsb.ap[:, 2:3], 1.0 / 128)
            TT(otile.ap, otile.ap, orstd.ap, ALU.mult, [otile.b, orstd.b], [otile.b])
            STT(ogt.ap[:, tsl], otile.ap, pcol("dng", 0, 1), zs.ap[:, tsl], ALU.mult, ALU.mult, [otile.b, par.b, zs.b], [ogt.b])
            if tt == NT - 1:
                Sd.dma("sp", og_s[h], ogt.ap, r=[ogt.b], w=[og_b[h]])
            yield

        def run_gen(g):
            if g is not None:
                for _ in g:
                    pass

        def interleave(ga, gb, ra=1, rb=1):
            gens = [[ga, ra, False], [gb, rb, False]]
            while not (gens[0][2] and gens[1][2]):
                for gi_, gsl in enumerate(gens):
                    for _ in range(gsl[1]):
                        if not gsl[2]:
                            try:
                                v = next(gsl[0])
                                if v == "BARRIER":
                                    oth = gens[1 - gi_]
                                    if not oth[2]:
                                        run_gen(oth[0])
                                        oth[2] = True
                            except StopIteration:
                                gsl[2] = True

        pending = None
        for h in range(H):
            gp = gen_proj(h)
            if pending is not None:
                interleave(pending, gp, 1, 1)
                pending = None
            else:
                run_gen(gp)
            MEMSET(S32.ap, 0.0, [S32.b])
            MEMSET(Sbp[0].ap, 0.0, [Sbp[0].b])
            run_gen(gen_p1(h, 0))
            for tt in range(NT):
                gs = gen_scan(h, tt)
                if tt + 1 < NT:
                    interleave(gs, gen_p1(h, tt + 1), 2, 3)
                else:
                    pending = gs
        run_gen(pending)
        for t in raw + acc + tnt + qkv + zs2[:1] + Sbp + [sqq] + sqql + kdec[:1] + attnTp + u0p + KTp + Hnp + qdecp + \
                [rsq, ogt, S32, kbg, wtm, vtm, coef, Amat, Xm, Xb, otile, orstd]:
            t.free()
        for t in hn + [rows, tabs, egl, tot, wba]:
            t.free()

        CKPT(5)
        ogr = [Tile(ar, f"ogr{k}", S, BF16) for k in range(8)]
        cfr = [Tile(ar, f"cfr{k}", S, BF16) for k in range(8)]
        for k in range(8):
            Sd.dma("sp", ogr[k].ap, og_s[k], r=[og_b[k]], w=[ogr[k].b])
            Sd.dma("sp", cfr[k].ap, cf_s[k], r=[cf_b[k]], w=[cfr[k].b])
        merged = [Tile(ar, f"mg{k}", S, BF16) for k in range(KC)]
        gl = [Tile(ar, f"gl{i}", S, BF16) for i in range(4)]
        mt = [Tile(ar, f"mt{i}", 512, F32) for i in range(4)]
        mi = 0
        for gq in range(8):
            wab = next_weight("wab")
            for n in range(2):
                nn = gq * 2 + n
                ta = gl[(nn % 2) * 2]
                tb = gl[(nn % 2) * 2 + 1]
                Sd.dma("sp", ta.ap, ga_s[nn], r=[ga_b[nn]], w=[ta.b])
                Sd.dma("sp", tb.ap, gb_s[nn], r=[gb_b[nn]], w=[tb.b])
                for tt in range(NT):
                    sl = slice(tt * 512, (tt + 1) * 512)
                    pa = bank(MMB)
                    for kc in range(8):
                        MM(ps[pa][:, :], wab.ap[:, kc * 256 + n * 128: kc * 256 + (n + 1) * 128], ogr[kc].ap[:, sl],
                           kc == 0, kc == 7, [wab.b, ogr[kc].b], [psb[pa]])
                    pb = bank(MMB)
                    for kc in range(8):
                        MM(ps[pb][:, :], wab.ap[:, (8 + kc) * 256 + n * 128: (8 + kc) * 256 + (n + 1) * 128], cfr[kc].ap[:, sl],
                           kc == 0, kc == 7, [wab.b, cfr[kc].b], [psb[pb]])
                    m1 = mt[mi % 4]
                    m2_ = mt[(mi + 1) % 4]
                    mi += 2
                    STT(m1.ap, ta.ap[:, sl], 1.0, ps[pa][:, :], ALU.add, ALU.mult, [ta.b, psb[pa]], [m1.b])
                    STT(m2_.ap, tb.ap[:, sl], 1.0, ps[pb][:, :], ALU.add, ALU.mult, [tb.b, psb[pb]], [m2_.b])
                    TT(merged[nn].ap[:, sl], m1.ap, m2_.ap, ALU.add, [m1.b, m2_.b], [merged[nn].b])
        for t in ogr + cfr + gl:
            t.free()
        x1t = [Tile(ar, f"x1t{i}", S, F32) for i in range(2)]
        alloc_xl()
        rstd = Tile(ar, "rstd2", S, F32)
        for g in range(8):
            wt = next_weight("wout")
            for n in range(2):
                nn = g * 2 + n
                xt = load_chunk(xT_d[nn * 128:(nn + 1) * 128, :], [])
                x1 = x1t[nn % 2]
                for tt in range(NT):
                    sl = slice(tt * 512, (tt + 1) * 512)
                    pb = bank(MMB)
                    for kc in range(KC):
                        MM(ps[pb][:, :], wt.ap[:, kc * 256 + n * 128: kc * 256 + (n + 1) * 128], merged[kc].ap[:, sl],
                           kc == 0, kc == KC - 1, [wt.b, merged[kc].b], [psb[pb]])
                    flush_deferred(1)
                    STT(x1.ap[:, sl], ps[pb][:, :], gt1q.ap[:, nn:nn + 1], xt.ap[:, sl], ALU.mult, ALU.add,
                        [psb[pb], gt1q.b, xt.b], [x1.b])
                    sumsq_accum(x1.ap[:, sl], [x1.b], tt, nn == 0, nn == 15, SB[tt], defer=True)
                Sd.dma("sp", x1_s[nn], x1.ap, r=[x1.b], w=[x1_b[nn]])
                if nn == 0:
                    DBG("x1_0", x1.ap[:, 0:512], [x1.b])
        flush_deferred()
        for tt in range(NT):
            rstd_from(SB[tt], rstd.ap[:, tt * 512:(tt + 1) * 512], [rstd.b], 1.0 / D)
        for t in merged + x1t + mt:
            t.free()

        CKPT(6)
        hn = [Tile(ar, f"hn2_{k}", S, BF16) for k in range(KC)]
        norm_modulate(lambda k: x1_s[k], lambda k: [x1_b[k]], gs2, 48, mod.bufs[1])
        graw = [Tile(ar, f"graw{i}", 4 + S, F32) for i in range(2)]
        for t in graw:
            MEMSET(t.ap[:, 0:4], 0.0, [t.b])
        hch = [Tile(ar, f"hch{i}", S, BF16) for i in range(2)]
        fa = [Tile(ar, f"fa{i}", 512, F32) for i in range(2)]
        ft = [Tile(ar, f"ft{i}", 512, F32) for i in range(2)]
        fi = 0
        for f in range(NF):
            wt = next_weight("wup")
            gr = graw[f % 2]
            hc = hch[f % 2]
            wofs = o_fcw + f * 3
            for tt in range(NT):
                sl = slice(tt * 512, (tt + 1) * 512)
                pg = bank(MMB)
                proj(wt, 0, 256, tt, pg)
                pu = bank(MMB)
                proj(wt, 128, 256, tt, pu)
                Sd.add("act", lambda e, o=gr.ap[:, 4 + tt * 512: 4 + (tt + 1) * 512], i=ps[pg][:, :]: e.activation(out=o, in_=i, func=AF.Copy),
                       [psb[pg]], [gr.b])
                a_ = fa[fi % 2]
                t_ = ft[fi % 2]
                fi += 1
                c0 = tt * 512 + 2
                TS(a_.ap, gr.ap[:, c0:c0 + 512], hw.ap[:, wofs:wofs + 1], None, ALU.mult, None, [gr.b, hw.b], [a_.b])
                for k in range(1, 3):
                    STT(a_.ap, gr.ap[:, c0 + k:c0 + k + 512], hw.ap[:, wofs + k:wofs + k + 1], a_.ap, ALU.mult, ALU.add,
                        [gr.b, hw.b, a_.b], [a_.b])
                ACT(t_.ap, a_.ap, AF.Tanh, [a_.b], [t_.b])
                STT(a_.ap, t_.ap, 1.0, a_.ap, ALU.add, ALU.mult, [t_.b, a_.b], [a_.b])
                TT(hc.ap[:, sl], a_.ap, ps[pu][:, :], ALU.mult, [a_.b, psb[pu]], [hc.b])
            Sd.dma("sp", h_s[f], hc.ap, r=[hc.b], w=[h_b[f]])
            if f == 0:
                DBG("h0", hc.ap[:, 0:512], [hc.b])
        for t in hn + graw + hch + fa + ft:
            t.free()
        CKPT(7)
        wsA.done()
        wsB = WStream(worderB, "wbufB", NF * 128, nb=3)
        cur_ws[0] = wsB
        free_xl()
        HS = min(S, 1024)
        NHT = HS // 512
        D2B = [0, 1, 2, 3, 6, 7]
        x1l = [Tile(ar, f"x1l{i}", HS, F32) for i in range(3)]
        x2t = [Tile(ar, f"x2t{i}", HS, F32) for i in range(3)]
        fxl = [Tile(ar, f"fxl{i}", HS, F32) for i in range(3)]
        fot = [Tile(ar, f"fot{i}", HS, F32) for i in range(3)]
        cnt = 0
        for half in range(S // HS):
            hs0 = half * HS
            hh = [Tile(ar, f"hh{half}_{f}", HS, BF16) for f in range(NF)]
            for f in range(NF):
                Sd.dma("sp", hh[f].ap, h_s[f][:, hs0:hs0 + HS], r=[h_b[f]], w=[hh[f].b])
            for n in range(16):
                wt = next_weight("wdn")
                xt = x1l[cnt % 3]
                x2 = x2t[cnt % 3]
                cnt += 1
                Sd.dma("sp", xt.ap, x1_s[n][:, hs0:hs0 + HS], r=[x1_b[n]], w=[xt.b])
                for tt in range(NHT):
                    sl = slice(tt * 512, (tt + 1) * 512)
                    pb = bank(D2B)
                    for f in range(NF):
                        MM(ps[pb][:, :], wt.ap[:, f * 128:(f + 1) * 128], hh[f].ap[:, sl], f == 0, f == NF - 1,
                           [wt.b, hh[f].b], [psb[pb]])
                    flush_deferred(1)
                    STT(x2.ap[:, sl], ps[pb][:, :], mod.ap[:, 80 + n: 81 + n], xt.ap[:, sl], ALU.mult, ALU.add,
                        [psb[pb], mod.bufs[1], xt.b], [x2.b])
                    sumsq_accum(x2.ap[:, sl], [x2.b], tt, n == 0, n == 15, SB[tt], defer=True)
                Sd.dma("sp", x2_s[n][:, hs0:hs0 + HS], x2.ap, r=[x2.b], w=[x2_b[n]])
            flush_deferred()
            for tt in range(NHT):
                rstd_from(SB[tt], rstd.ap[:, hs0 + tt * 512: hs0 + (tt + 1) * 512], [rstd.b], 1.0 / D)
            for n in range(16):
                xt = fxl[n % 3]
                Sd.dma("sp", xt.ap, x2_s[n][:, hs0:hs0 + HS], r=[x2_b[n]], w=[xt.b])
                o_ = fot[n % 3]
                STT(o_.ap, xt.ap, pcol("gf", n, 1), rstd.ap[:, hs0:hs0 + HS], ALU.mult, ALU.mult, [xt.b, par.b, rstd.b], [o_.b])
                Sd.out_dma.append(Sd.dma("sp", out_d[n * 128:(n + 1) * 128, hs0:hs0 + HS], o_.ap, r=[o_.b]))
            for t in hh:
                t.free()

        wsB.done()
    except _Stop:
        pass
    print("ops:", Sd.count(), "arena peak KiB:", ar.peak / 1024)
    if RESCHED:
        est = Sd.reschedule()
        print("rescheduled; simulated makespan us:", est / 1e3)
    Sd.emit(nc, st, final_wait_ops=Sd.out_dma)
    st.close()
    return nc, PC, CC


_CACHE = {}


def _run(inputs, S, FF, dbg=(), ncores=8, stage=99):
    inp = {k: np.asarray(v, np.float32) for k, v in inputs.items()}
    key = (S, FF, tuple((n_, tuple(s_)) for n_, s_ in dbg), stage)
    if key not in _CACHE:
        _CACHE[key] = build_program(S, FF, dbg, stage)
    nc, PC, CC = _CACHE[key]
    W = _prep_weights(inp, FF)
    consts = _const_cols(S).build()
    in_maps = []
    for b in range(ncores):
        m = {"xT": np.ascontiguousarray(inp["x"][b].T), "params": _param_cols(inp, b, FF).build(), "consts": consts,
             "seg": _seg(S)}
        m.update(W)
        in_maps.append(m)
    res = run_bass_kernel_spmd(nc, in_maps, core_ids=list(range(ncores)))
    return res


def kernel(**inputs):
    S = inputs["x"].shape[1]
    FF = inputs["ffn_w_down"].shape[1]
    B = inputs["x"].shape[0]
    res = _run(inputs, S, FF, ncores=B)
    out = np.stack([np.ascontiguousarray(res.results[b]["outT"].T) for b in range(B)], 0)
    return out.astype(np.float32)
```

```python
import numpy as np
from contextlib import ExitStack
import concourse.bass as bass
import concourse.mybir as mybir
from concourse.bass_utils import run_bass_kernel_spmd

F32 = mybir.dt.float32
BF16 = mybir.dt.bfloat16
AF = mybir.ActivationFunctionType
ALU = mybir.AluOpType

D = 2048
KC = D // 128
H = 8
EPS = 1e-6

RESCHED = True
import os as _osx
EPOCH = 6000
DMA_EPOCH = 1500
N_DMA_SEMS = 14


class Buf:
    __slots__ = ("name", "last_w", "readers", "psum")

    def __init__(self, name, psum=False):
        self.name = name
        self.psum = psum
        self.last_w = None
        self.readers = []


class Op:
    __slots__ = ("eng", "fn", "deps", "need_mark", "mark", "is_dma", "prev_same_sem", "odeps", "cost", "gi")

    def __init__(self, eng, fn, is_dma):
        self.eng = eng
        self.fn = fn
        self.odeps = []
        self.cost = 300.0
        self.gi = 0
        self.deps = []
        self.need_mark = is_dma
        self.mark = None
        self.is_dma = is_dma
        self.prev_same_sem = None


class Sched:
    ENGS = ("pe", "act", "dve", "pool", "sp")

    def __init__(self):
        self.ops = {e: [] for e in self.ENGS}
        self.all_dma = []
        self.out_dma = []
        self.glob = []

    def add(self, eng, fn, r=(), w=(), dma=False, cost=300.0):
        op = Op(eng, fn, dma)
        op.cost = cost
        op.gi = len(self.glob)
        self.glob.append(op)
        deps = []
        odeps = op.odeps
        seen = set()
        w = list(w) + [b for b in r if b.psum]
        r = [b for b in r if not b.psum]

        def dep(o):
            if o is None or id(o) in seen:
                return
            if o.eng == "pe" and eng == "pe" and not o.is_dma and not dma:
                seen.add(id(o))
                odeps.append(o)
                return
            seen.add(id(o))
            deps.append(o)

        for b in r:
            dep(b.last_w)
        for b in w:
            dep(b.last_w)
            for o in b.readers:
                dep(o)
        for o in deps:
            o.need_mark = True
        op.deps = deps
        for b in r:
            b.readers.append(op)
        for b in w:
            b.last_w = op
            b.readers = []
        self.ops[eng].append(op)
        if dma:
            self.all_dma.append(op)
        return op

    def dma(self, q, out, in_, r=(), w=()):
        n = 1
        for d in out.shape:
            n *= d
        return self.add(q, lambda e: e.dma_start(out=out, in_=in_), r, w, dma=True, cost=2000.0 + n * 4 / 250.0)

    def reschedule(self):
        import heapq
        ops = self.glob
        n = len(ops)
        import os as _os
        LAT = float(_os.environ.get("K_LAT", "350"))
        succ = [[] for _ in range(n)]
        indeg = [0] * n
        for op in ops:
            for d in op.deps + op.odeps:
                succ[d.gi].append(op.gi)
                indeg[op.gi] += 1
        blevel = [0.0] * n
        for i in range(n - 1, -1, -1):
            op = ops[i]
            b = 0.0
            for sidx in succ[i]:
                v = blevel[sidx] + (0.0 if (ops[sidx].eng == op.eng and not op.is_dma) else LAT)
                if v > b:
                    b = v
            blevel[i] = b + op.cost
        fut = {e: [] for e in self.ENGS}
        avl = {e: [] for e in self.ENGS}
        efree = {e: 0.0 for e in self.ENGS}
        finish = [0.0] * n
        dready = [0.0] * n
        for i in range(n):
            if indeg[i] == 0:
                heapq.heappush(fut[ops[i].eng], (0.0, -blevel[i], i))
        neworder = {e: [] for e in self.ENGS}
        why = [None] * n
        startt = [0.0] * n
        lastop = {e: None for e in self.ENGS}
        drsrc = [None] * n
        done = 0
        while done < n:
            best = None
            for e in self.ENGS:
                t = efree[e]
                if avl[e]:
                    cand_t = t
                elif fut[e]:
                    cand_t = max(t, fut[e][0][0])
                else:
                    continue
                if best is None or cand_t < best[0]:
                    best = (cand_t, e)
            t, e = best
            while fut[e] and fut[e][0][0] <= t:
                dr, nb, gi = heapq.heappop(fut[e])
                heapq.heappush(avl[e], (nb, gi))
            nb, gi = heapq.heappop(avl[e])
            op = ops[gi]
            start = max(t, dready[gi])
            startt[gi] = start
            if dready[gi] >= efree[e] and drsrc[gi] is not None:
                why[gi] = ("dep", drsrc[gi])
            elif lastop[e] is not None:
                why[gi] = ("eng", lastop[e])
            lastop[e] = gi
            if op.is_dma:
                efree[e] = start + 120.0
            else:
                efree[e] = start + op.cost
            finish[gi] = start + op.cost
            neworder[e].append(op)
            done += 1
            for sidx in succ[gi]:
                so = ops[sidx]
                lat = 0.0 if (so.eng == op.eng and not op.is_dma) else LAT
                v = finish[gi] + lat
                if v > dready[sidx]:
                    dready[sidx] = v
                    drsrc[sidx] = gi
                indeg[sidx] -= 1
                if indeg[sidx] == 0:
                    heapq.heappush(fut[so.eng], (dready[sidx], -blevel[sidx], sidx))
        self.ops = neworder
        import os as _os
        if _os.environ.get("K_CRIT"):
            cur = max(range(n), key=lambda i: finish[i])
            acc = {}
            hops = 0
            segs = []
            while cur is not None:
                op = ops[cur]
                k = op.eng + ("_dma" if op.is_dma else "")
                acc[k] = acc.get(k, 0.0) + op.cost
                segs.append((startt[cur], k, op.cost))
                w_ = why[cur]
                if w_ is None:
                    break
                if w_[0] == "dep":
                    hops += 1
                cur = w_[1]
            print("critical path: by engine us", {k: round(v / 1e3, 1) for k, v in acc.items()}, "dep hops", hops)
            import collections
            win = collections.defaultdict(lambda: collections.defaultdict(float))
            for st_, k, c in segs:
                win[int(st_ // 500000)][k] += c
            for wi in sorted(win):
                print("  t=%5.1fms" % (wi * 0.5), {k: round(v / 1e3) for k, v in win[wi].items()})
        return max(finish) if n else 0.0

    def count(self):
        return {e: len(v) for e, v in self.ops.items()}

    def emit(self, nc, stack, final_wait_ops=()):
        def new_sem(name):
            return stack.enter_context(nc.semaphore(name))

        for e in ("pe", "act", "dve", "pool"):
            cnt = 0
            sem = None
            k = 0
            for op in self.ops[e]:
                if op.is_dma or not op.need_mark:
                    continue
                if sem is None or cnt >= EPOCH:
                    sem = new_sem(f"s_{e}_{k}")
                    k += 1
                    cnt = 0
                cnt += 1
                op.mark = (sem, cnt)
        for q in ("sp", "pool"):
            dmas = [op for op in self.ops[q] if op.is_dma]
            slots = [[None, 0, None, 0] for _ in range(N_DMA_SEMS)]
            for i, op in enumerate(dmas):
                sl = slots[i % N_DMA_SEMS]
                if sl[0] is None or sl[1] >= DMA_EPOCH:
                    sl[0] = new_sem(f"d_{q}_{i % N_DMA_SEMS}_{sl[3]}")
                    sl[3] += 1
                    sl[1] = 0
                    sl[2] = None
                sl[1] += 1
                op.mark = (sl[0], 16 * sl[1])
                op.prev_same_sem = sl[2]
                sl[2] = op

        final_wait_ops = list(final_wait_ops)
        engmap = {"pe": "tensor", "act": "scalar", "dve": "vector", "pool": "gpsimd", "sp": "sync"}
        sched = self

        def run_engine(ename):
            def body(e):
                waited = {}

                def wait_for(o):
                    sem, val = o.mark
                    key = id(sem)
                    if waited.get(key, 0) >= val:
                        return
                    e.wait_ge(sem, val)
                    waited[key] = val

                for op in sched.ops[ename]:
                    for d in op.deps:
                        wait_for(d)
                    if op.is_dma and op.prev_same_sem is not None:
                        wait_for(op.prev_same_sem)
                    ins = op.fn(e)
                    if op.mark is not None:
                        ins.then_inc(op.mark[0], 16 if op.is_dma else 1)
                if ename == "sp":
                    for o in final_wait_ops:
                        wait_for(o)
            return body

        with nc.Block() as block:
            for ename in self.ENGS:
                getattr(block, engmap[ename])(run_engine(ename))


class Arena:
    def __init__(self, nc, stack, nbytes):
        self.nbytes = nbytes
        self.t = stack.enter_context(nc.sbuf_tensor("arena", [128, nbytes // 4], F32))
        self.live = {}
        self.hist = []
        self.peak = 0

    def _find(self, n):
        spans = sorted((s, e) for s, e, _ in self.live.values())
        pos = 0
        for s, e in spans:
            if s - pos >= n:
                return pos
            pos = max(pos, e)
        if self.nbytes - pos >= n:
            return pos
        raise MemoryError(f"arena full: need {n}, live={sorted((v[0], v[1], k) for k, v in self.live.items())}")

    def alloc(self, name, nbytes, nbufs=1):
        nbytes = (nbytes + 63) // 64 * 64
        s = self._find(nbytes)
        e = s + nbytes
        self.peak = max(self.peak, e)
        bufs = [Buf(f"{name}{i}") for i in range(nbufs)]
        inh = []
        keep = []
        for (hs, he, hb) in self.hist:
            if hs < e and s < he:
                inh.extend(hb)
                if hs < s or he > e:
                    keep.append((hs, he, hb))
            else:
                keep.append((hs, he, hb))
        self.hist = keep
        for b in bufs:
            for hb in inh:
                if hb.last_w is not None:
                    b.readers.append(hb.last_w)
                b.readers.extend(hb.readers)
        assert name not in self.live, name
        self.live[name] = (s, e, bufs)
        return s, bufs

    def free(self, name):
        s, e, bufs = self.live.pop(name)
        self.hist.append((s, e, bufs))

    def view(self, off, ncols, dtype):
        assert off % 4 == 0
        if dtype == F32:
            return self.t[:, off // 4: off // 4 + ncols]
        assert ncols % 2 == 0
        return self.t[:, off // 4: off // 4 + ncols // 2].bitcast(BF16)


class Tile:
    def __init__(self, arena, name, ncols, dtype, nbufs=1):
        self.arena = arena
        self.name = name
        sz = 4 if dtype == F32 else 2
        self.off, self.bufs = arena.alloc(name, ncols * sz, nbufs)
        self.ap = arena.view(self.off, ncols, dtype)
        self.b = self.bufs[0]

    def free(self):
        self.arena.free(self.name)


def _relayout(W, gw):
    K, N = W.shape
    kc = K // 128
    g = N // gw
    return np.ascontiguousarray(W.reshape(kc, 128, g, gw).transpose(2, 1, 0, 3).reshape(g, 128, kc * gw))


def _pervec(v):
    return np.ascontiguousarray(v.reshape(-1, 128).T)


class _Cols:
    def __init__(self):
        self.n = 0
        self.off = {}
        self.parts = []

    def add(self, name, arr):
        arr = np.asarray(arr, np.float32)
        if arr.shape[0] < 128:
            arr = np.concatenate([arr, np.zeros((128 - arr.shape[0], arr.shape[1]), np.float32)], 0)
        self.off[name] = (self.n, arr.shape[1])
        self.n += arr.shape[1]
        self.parts.append(arr)

    def build(self):
        return np.ascontiguousarray(np.concatenate(self.parts, 1))


def _const_cols(S):
    C = _Cols()
    C.add("ident", np.eye(128, dtype=np.float32))
    x = np.arange(64)[:, None]
    y = np.arange(64)[None, :]
    C.add("sgn", np.where(y >= x, 1.0, -1.0))
    C.add("mge", (y >= x).astype(np.float32))
    C.add("mlt", (y < x).astype(np.float32))
    C.add("mgt", (y > x).astype(np.float32))
    sel64 = np.zeros((8, 8 * 64), np.float32)
    sel128 = np.zeros((8, 8 * 128), np.float32)
    for h in range(8):
        sel64[h, h * 64:(h + 1) * 64] = 1
        sel128[h, h * 128:(h + 1) * 128] = 1
    C.add("sel64", sel64)
    C.add("sel128", sel128)
    return C


def _seg(S):
    seg = np.ones((8, S), np.float32)
    seg[:, ::64] = 0
    return seg


def _param_cols(inp, b, FF):
    P = _Cols()
    P.add("c", _pervec(inp["c"][b]))
    P.add("b_ada", _pervec(inp["b_ada"][0]))
    P.add("g1", _pervec(inp["norm1_g"][0]))
    P.add("g2", _pervec(inp["norm2_g"][0]))
    P.add("gf", _pervec(inp["final_norm_g"]))
    dcw = inp["dn_conv_w"][0]
    P.add("dcw", dcw.reshape(4, 24, 128).transpose(2, 1, 0).reshape(128, 96))
    ccw = inp["cf_conv_w"][0]
    P.add("ccw", ccw.reshape(31, 8, 128).transpose(2, 1, 0).reshape(128, 8 * 31))
    P.add("lng", _pervec(inp["cf_ln_g"][0]))
    P.add("lnb", _pervec(inp["cf_ln_b"][0]))
    fcw = inp["ffn_conv_w"][0]
    nf = FF // 128
    P.add("fcw", fcw.reshape(3, nf, 128).transpose(2, 1, 0).reshape(128, nf * 3))
    P.add("dng", inp["dn_norm_g"][0].reshape(128, 1))
    P.add("alog", inp["dn_a_log"][0].reshape(8, 1))
    P.add("dtb", inp["dn_dt_bias"][0].reshape(8, 1))
    return P


def _prep_weights(inp, FF):
    w_in = inp["w_in"][0]
    o_z, o_b, o_a, o_glu, o_ga, o_gb = 3072, 4096, 4104, 4112, 6160, 8208
    cols = []
    for j in range(8):
        cols += list(range(o_glu + j * 128, o_glu + (j + 1) * 128))
        cols += list(range(o_glu + 1024 + j * 128, o_glu + 1024 + (j + 1) * 128))
    cols += list(range(o_ga, o_ga + 2048))
    cols += list(range(o_gb, o_gb + 2048))
    for h in range(8):
        cols += list(range(h * 128, (h + 1) * 128))
        cols += list(range(1024 + h * 128, 1024 + (h + 1) * 128))
        cols += list(range(2048 + h * 128, 2048 + (h + 1) * 128))
        cols += list(range(o_z + h * 128, o_z + (h + 1) * 128))
    cols = np.asarray(cols)
    W = {}
    W["win"] = _relayout(w_in[:, cols], 256)
    W["wba"] = _relayout(w_in[:, o_b:o_b + 16], 16)
    W["wada"] = _relayout(inp["w_ada"][0], 256)
    W["wab"] = _relayout(np.concatenate([inp["dn_w_o"][0], inp["cf_w_o"][0]], 0), 256)
    W["wout"] = _relayout(inp["w_out"][0], 256)
    wup = inp["ffn_w_up"][0]
    nf = FF // 128
    ucols = []
    for f in range(nf):
        ucols += list(range(f * 128, (f + 1) * 128))
        ucols += list(range(FF + f * 128, FF + (f + 1) * 128))
    W["wup"] = _relayout(wup[:, np.asarray(ucols)], 256)
    W["wdn"] = _relayout(inp["ffn_w_down"][0], 128)
    return W


class _Stop(Exception):
    pass


def build_program(S, FF, dbg=(), stage=99):
    NT = S // 512
    NCH = S // 64
    NF = FF // 128
    nc = bass.Bass("TRN2", target_bir_lowering=False)
    CC = _const_cols(S)
    dummy = {"c": np.zeros((1, D), np.float32), "b_ada": np.zeros((1, 6 * D), np.float32),
             "norm1_g": np.zeros((1, D), np.float32), "norm2_g": np.zeros((1, D), np.float32),
             "final_norm_g": np.zeros((D,), np.float32), "dn_conv_w": np.zeros((1, 4, 3072), np.float32),
             "cf_conv_w": np.zeros((1, 31, 1024), np.float32), "cf_ln_g": np.zeros((1, 1024), np.float32),
             "cf_ln_b": np.zeros((1, 1024), np.float32), "ffn_conv_w": np.zeros((1, 3, FF), np.float32),
             "dn_norm_g": np.zeros((1, 128), np.float32), "dn_a_log": np.zeros((1, 8), np.float32),
             "dn_dt_bias": np.zeros((1, 8), np.float32)}
    PC = _param_cols(dummy, 0, FF)
    NPAR, NCON = PC.n, CC.n

    def din(name, shape):
        return nc.dram_tensor(name, list(shape), F32, kind="ExternalInput").ap()

    xT_d = din("xT", [D, S])
    par_d = din("params", [128, NPAR])
    con_d = din("consts", [128, NCON])
    seg_d = din("seg", [8, S])
    wd = {"win": din("win", [40, 128, 4096]), "wba": din("wba", [1, 128, 256]),
          "wada": din("wada", [48, 128, 4096]), "wab": din("wab", [8, 128, 4096]), "wout": din("wout", [8, 128, 4096]),
          "wup": din("wup", [NF, 128, 4096]), "wdn": din("wdn", [16, 128, NF * 128])}
    out_d = nc.dram_tensor("outT", [D, S], F32, kind="ExternalOutput").ap()

    def scratch(name, shape, dt):
        return nc.dram_tensor(name, list(shape), dt, kind="Internal").ap()

    ga_s = scratch("ga_s", [16, 128, S], BF16)
    gb_s = scratch("gb_s", [16, 128, S], BF16)
    cf_s = scratch("cf_s", [8, 128, S], BF16)
    og_s = scratch("og_s", [8, 128, S], BF16)
    x1_s = scratch("x1_s", [16, 128, S], F32)
    x2_s = scratch("x2_s", [16, 128, S], F32)
    h_s = scratch("h_s", [NF, 128, S], BF16)
    ga_b = [Buf(f"ga_s{i}") for i in range(16)]
    gb_b = [Buf(f"gb_s{i}") for i in range(16)]
    cf_b = [Buf(f"cf_s{i}") for i in range(8)]
    og_b = [Buf(f"og_s{i}") for i in range(8)]
    x1_b = [Buf(f"x1_s{i}") for i in range(16)]
    x2_b = [Buf(f"x2_s{i}") for i in range(16)]
    h_b = [Buf(f"h_s{i}") for i in range(NF)]
    dbg_out = {}
    for nm, shp in dbg:
        dbg_out[nm] = nc.dram_tensor("dbg_" + nm, list(shp), F32, kind="ExternalOutput").ap()

    Sd = Sched()
    st = ExitStack()
    ar = Arena(nc, st, 206 * 1024)
    ps = [st.enter_context(nc.psum_tensor(f"ps{i}", [128, 512], F32)) for i in range(8)]
    psb = [Buf(f"ps{i}", psum=True) for i in range(8)]
    rot = {}

    def bank(pool):
        i = rot.get(id(pool), 0)
        rot[id(pool)] = i + 1
        return pool[i % len(pool)]

    def _fsz(ap):
        n = 1
        for d in ap.shape[1:]:
            n *= d
        return n

    def MM(out, lhsT, rhs, start, stop, r, w):
        c = max(_fsz(out) / 2.4 + 10.0, 56.0)
        if lhsT.dtype == F32:
            c *= 4.0
        Sd.add("pe", lambda e: e.matmul(out, lhsT=lhsT, rhs=rhs, start=start, stop=stop), r, w, cost=c)

    def ACT(out, in_, func, r, w, bias=0.0, scale=1.0):
        Sd.add("act", lambda e: e.activation(out=out, in_=in_, func=func, bias=bias, scale=scale), r, w,
               cost=(_fsz(out) + 300.0) / 1.2)

    _WHATIF = {"scale": 1.0}

    def _vc(out, eng="dve"):
        c = (_fsz(out) + 170.0) / 0.96 * _WHATIF["scale"]
        return c * 2.6 if eng == "pool" else c

    def STT(out, in0, scalar, in1, op0, op1, r, w, eng="dve"):
        Sd.add(eng, lambda e: e.scalar_tensor_tensor(out=out, in0=in0, scalar=scalar, in1=in1, op0=op0, op1=op1), r, w, cost=_vc(out))

    def TT(out, in0, in1, op, r, w, eng="dve"):
        Sd.add(eng, lambda e: e.tensor_tensor(out=out, in0=in0, in1=in1, op=op), r, w, cost=_vc(out, eng))

    def TS(out, in0, s1, s2, op0, op1, r, w, eng="dve"):
        if s2 is None:
            Sd.add(eng, lambda e: e.tensor_scalar(out=out, in0=in0, scalar1=s1, scalar2=None, op0=op0), r, w, cost=_vc(out))
        else:
            Sd.add(eng, lambda e: e.tensor_scalar(out=out, in0=in0, scalar1=s1, scalar2=s2, op0=op0, op1=op1), r, w, cost=_vc(out))

    def CP(out, in_, r, w, eng="dve"):
        Sd.add(eng, lambda e: e.tensor_copy(out=out, in_=in_), r, w, cost=_vc(out))

    import os as _os2
    _RC = float(_os2.environ.get("K_RCOST", "6.5"))

    def RECIP(out, in_, r, w):
        Sd.add("dve", lambda e: e.reciprocal(out=out, in_=in_), r, w, cost=_fsz(out) * _RC + 150.0)

    USE_LNEXP = _os2.environ.get("K_LNEXP", "1") == "1"

    def RSQRT(dst, src, r, w, bias, scale=1.0):
        if USE_LNEXP:
            ACT(dst, src, AF.Ln, r, w, bias=bias, scale=scale)
            ACT(dst, dst, AF.Exp, w, w, scale=-0.5)
        else:
            ACT(dst, src, AF.Sqrt, r, w, bias=bias, scale=scale)
            RECIP(dst, dst, w, w)

    def MEMSET(ap, val, w, eng="dve"):
        Sd.add(eng, lambda e: e.memset(ap, val), (), w, cost=_vc(ap))

    def DBG(name, ap, r):
        if name in dbg_out:
            t = Tile(ar, "dbg_" + name, ap.shape[-1] if len(ap.shape) == 2 else int(np.prod(ap.shape[1:])), F32)
            np_ = ap.shape[0]
            v = t.ap[0:np_, :]
            CP(v, ap, r, [t.b])
            Sd.out_dma.append(Sd.dma("sp", dbg_out[name][0:np_, :], v, r=[t.b]))

    def CKPT(n):
        if stage <= n:
            raise _Stop()

    try:
        par = Tile(ar, "par", NPAR, F32)
        con = Tile(ar, "con", NCON, F32)
        Sd.dma("sp", par.ap, par_d, w=[par.b])
        Sd.dma("sp", con.ap, con_d, w=[con.b])

        def pcol(name, i=0, n=1, rows=128):
            o, _ = PC.off[name]
            return par.ap[0:rows, o + i: o + i + n]

        def ccol(name, i, n, rows=128):
            o, _ = CC.off[name]
            return con.ap[0:rows, o + i: o + i + n]

        identb = Tile(ar, "identb", 128, BF16)
        onesb = Tile(ar, "onesb", 128, BF16)
        CP(identb.ap, ccol("ident", 0, 128), [con.b], [identb.b])
        MEMSET(onesb.ap, 1.0, [onesb.b])
        mod = Tile(ar, "mod", 96, F32, nbufs=2)
        gs1 = Tile(ar, "gs1", 16, F32)
        gs2 = Tile(ar, "gs2", 16, F32)
        gt1q = Tile(ar, "gt1q", 16, F32)
        hw = Tile(ar, "hw", 96 + 248 + NF * 3 + 16, F32)
        o_dcw, o_ccw, o_fcw, o_hg, o_hb = 0, 96, 96 + 248, 96 + 248 + NF * 3, 96 + 248 + NF * 3 + 8
        TS(hw.ap[:, o_dcw:o_dcw + 96], pcol("dcw", 0, 96), 0.5, None, ALU.mult, None, [par.b], [hw.b])
        TS(hw.ap[:, o_ccw:o_ccw + 248], pcol("ccw", 0, 248), 0.5, None, ALU.mult, None, [par.b], [hw.b])
        TS(hw.ap[:, o_fcw:o_fcw + NF * 3], pcol("fcw", 0, NF * 3), 0.5, None, ALU.mult, None, [par.b], [hw.b])
        TS(hw.ap[:, o_hg:o_hg + 8], pcol("lng", 0, 8), 0.5, None, ALU.mult, None, [par.b], [hw.b])
        TS(hw.ap[:, o_hb:o_hb + 8], pcol("lnb", 0, 8), 0.5, None, ALU.mult, None, [par.b], [hw.b])

        NB = 2
        worderA = []
        worderA += [("win", g, 4096) for g in range(40)]
        worderAda1 = [("wada", g, 4096) for g in range(16)]
        worderAda2 = [("wada", g, 4096) for g in range(16, 48)]
        worderA += [("wab", g, 4096) for g in range(8)]
        worderA += [("wout", g, 4096) for g in range(8)]
        worderA += [("wup", f, 4096) for f in range(NF)]
        worderB = []
        for half in range(max(1, S // 1024)):
            worderB += [("wdn", n, NF * 128) for n in range(16)]

        class WStream:
            def __init__(self, order, name, elems, nb=NB):
                self.order = order
                self.nb = nb
                self.bufs = [Tile(ar, f"{name}{i}", elems, BF16) for i in range(nb)]
                self.issued = 0
                self.next = 0

            def get(self, expect):
                i = self.next
                assert self.order[i][0] == expect, (self.order[i], expect)
                while self.issued < min(len(self.order), i + self.nb):
                    nm, g, n = self.order[self.issued]
                    t = self.bufs[self.issued % self.nb]
                    Sd.dma("pool", t.ap[:, 0:n], wd[nm][g], w=[t.b])
                    self.issued += 1
                self.next = i + 1
                return self.bufs[i % self.nb]

            def done(self):
                assert self.next == len(self.order)
                for t in self.bufs:
                    t.free()

        wsA = WStream(worderA, "wbufA", 4096)
        wsAda = [WStream(worderAda1, "wbufAda", 4096)]
        cur_ws = [wsA]

        def next_weight(expect):
            return cur_ws[0].get(expect)

        cact = Tile(ar, "cact", 16, BF16)
        tmp16 = Tile(ar, "tmp16", 16, F32)
        ACT(tmp16.ap, pcol("c", 0, 16), AF.Tanh, [par.b], [tmp16.b], scale=0.5)
        STT(tmp16.ap, tmp16.ap, 1.0, pcol("c", 0, 16), ALU.add, ALU.mult, [tmp16.b, par.b], [tmp16.b])
        TS(cact.ap, tmp16.ap, 0.5, None, ALU.mult, None, [tmp16.b], [cact.b])

        def ada_mm(g, pbank):
            wt = wsAda[0].get("wada")
            for n in range(2):
                chunk = g * 2 + n
                for kc in range(KC):
                    MM(ps[pbank][:, chunk:chunk + 1], wt.ap[:, kc * 256 + n * 128: kc * 256 + n * 128 + 128],
                       cact.ap[:, kc:kc + 1], kc == 0, kc == KC - 1, [wt.b, cact.b], [psb[pbank]])

        def ada_fin(g0, g1, pbank, mbuf):
            c0, c1 = g0 * 2, g1 * 2
            TT(mod.ap[:, c0:c1], ps[pbank][:, c0:c1], pcol("b_ada", c0, c1 - c0), ALU.add, [psb[pbank], par.b], [mbuf])

        def ada_groups(g0, g1, pbank, mbuf):
            for g in range(g0, g1):
                ada_mm(g, pbank)
            ada_fin(g0, g1, pbank, mbuf)


        xl = []
        xl_i = [0]
        gen = [0]

        def alloc_xl(nbuf=2):
            gen[0] += 1
            xl[:] = [Tile(ar, f"xl{gen[0]}_{i}", S, F32) for i in range(nbuf)]

        def free_xl():
            for t in xl:
                t.free()
            xl[:] = []

        alloc_xl(6)

        def load_chunk(src_ap, srcbufs):
            t = xl[xl_i[0] % len(xl)]
            xl_i[0] += 1
            Sd.dma("sp", t.ap, src_ap, r=srcbufs, w=[t.b])
            return t

        sqb = [Tile(ar, f"sqb{i}", 512, BF16) for i in range(3)]
        sq_i = [0]

        deferred = []

        def flush_deferred(keep=0):
            while len(deferred) > keep:
                deferred.pop(0)()

        def sumsq_accum(src_ap, srcbufs, tt, first, last, pbank, defer=False):
            t = sqb[sq_i[0] % len(sqb)]
            sq_i[0] += 1
            ACT(t.ap, src_ap, AF.Square, srcbufs, [t.b])
            fn = lambda: MM(ps[pbank][:, :], onesb.ap, t.ap, first, last, [onesb.b, t.b], [psb[pbank]])
            if defer:
                deferred.append(fn)
            else:
                fn()

        def rstd_from(pbank, dst_ap, dstbufs, inv_n, scale_extra=1.0):
            RSQRT(dst_ap, ps[pbank][:, :], [psb[pbank]], dstbufs, epsb.ap[:, 0:1], inv_n)

        epsb = Tile(ar, "epsb", 4, F32)
        MEMSET(epsb.ap[:, 0:1], EPS, [epsb.b])
        MEMSET(epsb.ap[:, 1:2], 1.0, [epsb.b])
        MEMSET(epsb.ap[:, 2:3], 128.0 * EPS * EPS, [epsb.b])

        hn = [Tile(ar, f"hn{k}", S, BF16) for k in range(KC)]
        rstd = Tile(ar, "rstd", S, F32)

        def norm_modulate(src_of, srcbufs_of, gs, sh_c0, modbuf):
            gen[0] += 1
            ntmp = [Tile(ar, f"ntmp{gen[0]}_{i}", S, F32) for i in range(2)]
            for k in range(KC):
                xt = load_chunk(src_of(k), srcbufs_of(k))
                tm = ntmp[k % 2]
                STT(tm.ap, xt.ap, gs.ap[:, k:k + 1], rstd.ap, ALU.mult, ALU.mult, [xt.b, gs.b, rstd.b], [tm.b])
                ACT(hn[k].ap, tm.ap, AF.Identity, [tm.b, modbuf], [hn[k].b], bias=mod.ap[:, sh_c0 + k: sh_c0 + k + 1])
            for t in ntmp:
                t.free()

        SB = [4, 5, 6, 7]
        for k in range(KC):
            xt = load_chunk(xT_d[k * 128:(k + 1) * 128, :], [])
            for tt in range(NT):
                sumsq_accum(xt.ap[:, tt * 512:(tt + 1) * 512], [xt.b], tt, k == 0, k == KC - 1, SB[tt])
        for tt in range(NT):
            rstd_from(SB[tt], rstd.ap[:, tt * 512:(tt + 1) * 512], [rstd.b], 1.0 / D)
        ada_groups(0, 16, 0, mod.bufs[0])
        STT(gs1.ap, mod.ap[:, 16:32], 1.0, pcol("g1", 0, 16), ALU.add, ALU.mult, [mod.bufs[0], par.b], [gs1.b])
        wsAda[0].done()
        norm_modulate(lambda k: xT_d[k * 128:(k + 1) * 128, :], lambda k: [], gs1, 0, mod.bufs[0])
        DBG("hn0", hn[0].ap[:, 0:512], [hn[0].b])
        CKPT(1)
        free_xl()
        rstd.free()

        hn_bufs = [t.b for t in hn]

        def proj(wt, col0, gw, tt, pbank, rows=128, ncols=128):
            for kc in range(KC):
                MM(ps[pbank][0:ncols, :], wt.ap[:, kc * gw + col0: kc * gw + col0 + ncols],
                   hn[kc].ap[:, tt * 512:(tt + 1) * 512], kc == 0, kc == KC - 1, [wt.b, hn[kc].b], [psb[pbank]])

        MMB = [int(c) for c in _osx.environ.get("K_MMB", "0123")]
        AUX = [4, 5, 6, 7]

        wba = Tile(ar, "wba", 256, BF16)
        Sd.dma("pool", wba.ap, wd["wba"][0], w=[wba.b])
        rows = Tile(ar, "rows", 2 * S, F32)
        rowsB = Tile(ar, "rowsB", 2 * S, F32)
        R_BETA, R_GC = [rows.ap[0:8, i * S:(i + 1) * S] for i in range(2)]
        R_T1, R_T2 = [rowsB.ap[0:8, i * S:(i + 1) * S] for i in range(2)]
        R_G = R_T1
        R_DL = R_T1
        nA = Tile(ar, "nA", 2, F32)
        ACT(nA.ap[0:8, 0:1], pcol("alog", 0, 1, rows=8), AF.Exp, [par.b], [nA.b])
        TS(nA.ap[0:8, 0:1], nA.ap[0:8, 0:1], -1.0, None, ALU.mult, None, [nA.b], [nA.b])
        for tt in range(NT):
            sl = slice(tt * 512, (tt + 1) * 512)
            pb = bank(MMB)
            for kc in range(KC):
                MM(ps[pb][0:8, :], wba.ap[:, kc * 16: kc * 16 + 8], hn[kc].ap[:, sl], kc == 0, kc == KC - 1,
                   [wba.b, hn[kc].b], [psb[pb]])
            ACT(R_T2[:, sl], ps[pb][0:8, :], AF.Tanh, [psb[pb]], [rowsB.b], scale=0.5)
            TS(R_BETA[:, sl], R_T2[:, sl], 0.5, 0.5, ALU.mult, ALU.add, [rowsB.b], [rows.b])
            pb2 = bank(MMB)
            for kc in range(KC):
                MM(ps[pb2][0:8, :], wba.ap[:, kc * 16 + 8: kc * 16 + 16], hn[kc].ap[:, sl], kc == 0, kc == KC - 1,
                   [wba.b, hn[kc].b], [psb[pb2]])
            ACT(R_T2[:, sl], ps[pb2][0:8, :], AF.Exp, [psb[pb2], par.b], [rowsB.b], bias=pcol("dtb", 0, 1, rows=8))
            ACT(R_T2[:, sl], R_T2[:, sl], AF.Ln, [rowsB.b], [rowsB.b], bias=epsb.ap[0:8, 1:2])
            TS(R_G[:, sl], R_T2[:, sl], nA.ap[0:8, 0:1], None, ALU.mult, None, [rowsB.b, nA.b], [rowsB.b])
        Sd.dma("sp", R_T2, seg_d, w=[rowsB.b])
        Sd.add("dve", lambda e: e.tensor_tensor_scan(out=R_GC, data0=R_T2, data1=R_G, initial=0.0,
                                                      op0=ALU.mult, op1=ALU.add), [rowsB.b], [rows.b])
        tot = Tile(ar, "tot", NCH, F32)
        CP(tot.ap[0:8, :], R_GC.rearrange("p (n c) -> p n c", c=64)[:, :, 63], [rows.b], [tot.b])
        TT(R_DL.rearrange("p (n c) -> p n c", c=64), tot.ap[0:8, :].unsqueeze(2).to_broadcast([8, NCH, 64]),
           R_GC.rearrange("p (n c) -> p n c", c=64), ALU.subtract, [tot.b, rows.b], [rowsB.b])
        tabs = Tile(ar, "tabs", 4 * NCH * 8, F32)
        NC8 = NCH * 8
        T_GC, T_BETA, T_BEG, T_EDL = [tabs.ap[0:64, i * NC8:(i + 1) * NC8] for i in range(4)]
        for (src, dst) in ((R_GC, T_GC), (R_BETA, T_BETA), (R_DL, T_EDL)):
            for c0 in range(0, NCH, 64):
                pb = bank(AUX)
                nn = min(64, NCH - c0)
                for n in range(nn):
                    MM(ps[pb][0:64, n * 8:(n + 1) * 8], src[:, (c0 + n) * 64:(c0 + n + 1) * 64], ccol("ident", 0, 8, rows=8),
                       True, True, [rows.b, rowsB.b, con.b], [psb[pb]])
                CP(dst[:, c0 * 8:(c0 + nn) * 8], ps[pb][0:64, 0:nn * 8], [psb[pb]], [tabs.b])
        ACT(T_EDL, T_EDL, AF.Exp, [tabs.b], [tabs.b])
        ACT(T_BEG, T_GC, AF.Exp, [tabs.b], [tabs.b])
        TT(T_BEG, T_BEG, T_BETA, ALU.mult, [tabs.b], [tabs.b])
        egl = Tile(ar, "egl", 8 * NCH, F32)
        for h in range(H):
            pb = bank(AUX)
            MM(ps[pb][:, 0:NCH], ccol("sel128", h * 128, 128, rows=8), tot.ap[0:8, :], True, True, [con.b, tot.b], [psb[pb]])
            ACT(egl.ap[:, h * NCH:(h + 1) * NCH], ps[pb][:, 0:NCH], AF.Exp, [psb[pb]], [egl.b])
        DBG("rows", rows.ap[0:8, 0:2 * S], [rows.b])
        rowsB.free()
        CKPT(2)
        DBG("tabs", tabs.ap[0:64, :], [tabs.b])

        upad = [Tile(ar, f"upad{i}", 32 + S, BF16) for i in range(2)]
        for t in upad:
            MEMSET(t.ap[:, 0:32], 0.0, [t.b])
        diag = [Tile(ar, f"diag{i}", 31 * 128, BF16) for i in range(2)]
        uc = [Tile(ar, f"uc{j}", S, BF16) for j in range(8)]
        csum = Tile(ar, "csum", S, F32)
        csq = Tile(ar, "csq", S, F32)
        tg = [Tile(ar, f"tg{i}", 512, F32) for i in range(2)]
        tg_i = [0]
        for j in range(8):
            wt = next_weight("win")
            up = upad[j % 2]
            dg = diag[j % 2]
            TT(dg.ap.rearrange("p (k c) -> p k c", c=128),
               identb.ap.unsqueeze(1).to_broadcast([128, 31, 128]),
               hw.ap[:, o_ccw + j * 31: o_ccw + (j + 1) * 31].unsqueeze(2).to_broadcast([128, 31, 128]),
               ALU.mult, [identb.b, hw.b], [dg.b])
            for tt in range(NT):
                pv = bank(MMB)
                proj(wt, 0, 256, tt, pv)
                pg = bank(MMB)
                proj(wt, 128, 256, tt, pg)
                t_ = tg[tg_i[0] % 2]
                tg_i[0] += 1
                ACT(t_.ap, ps[pg][:, :], AF.Tanh, [psb[pg]], [t_.b], scale=0.5)
                STT(up.ap[:, 32 + tt * 512: 32 + (tt + 1) * 512], t_.ap, 1.0, ps[pv][:, :], ALU.add, ALU.mult,
                    [t_.b, psb[pv]], [up.b])
            for tt in range(NT):
                pc = bank(AUX)
                for k in range(31):
                    MM(ps[pc][:, :], dg.ap[:, k * 128:(k + 1) * 128], up.ap[:, 2 + k + tt * 512: 2 + k + tt * 512 + 512],
                       k == 0, k == 30, [dg.b, up.b], [psb[pc]])
                sl = slice(tt * 512, (tt + 1) * 512)
                CP(uc[j].ap[:, sl], ps[pc][:, :], [psb[pc]], [uc[j].b], eng="act" if False else "dve")
                sq_t = sqb[sq_i[0] % 2]
                sq_i[0] += 1
                ACT(sq_t.ap, ps[pc][:, :], AF.Square, [psb[pc]], [sq_t.b])
                p1 = bank(AUX)
                MM(ps[p1][:, :], onesb.ap, uc[j].ap[:, sl], True, True, [onesb.b, uc[j].b], [psb[p1]])
                if j == 0:
                    CP(csum.ap[:, sl], ps[p1][:, :], [psb[p1]], [csum.b])
                else:
                    TT(csum.ap[:, sl], csum.ap[:, sl], ps[p1][:, :], ALU.add, [csum.b, psb[p1]], [csum.b])
                p2 = bank(AUX)
                MM(ps[p2][:, :], onesb.ap, sq_t.ap, True, True, [onesb.b, sq_t.b], [psb[p2]])
                if j == 0:
                    CP(csq.ap[:, sl], ps[p2][:, :], [psb[p2]], [csq.b])
                else:
                    TT(csq.ap[:, sl], csq.ap[:, sl], ps[p2][:, :], ALU.add, [csq.b, psb[p2]], [csq.b])
        for t in upad + diag:
            t.free()
        lnt = [Tile(ar, f"lnt{i}", 512, F32) for i in range(2)]
        TS(csum.ap, csum.ap, 1.0 / 1024, None, ALU.mult, None, [csum.b], [csum.b])
        TS(csq.ap, csq.ap, 1.0 / 1024, None, ALU.mult, None, [csq.b], [csq.b])
        for tt in range(NT):
            sl = slice(tt * 512, (tt + 1) * 512)
            m2 = lnt[0]
            TT(m2.ap, csum.ap[:, sl], csum.ap[:, sl], ALU.mult, [csum.b], [m2.b])
            TT(csq.ap[:, sl], csq.ap[:, sl], m2.ap, ALU.subtract, [csq.b, m2.b], [csq.b])
        RSQRT(csq.ap, csq.ap, [csq.b], [csq.b], epsb.ap[:, 0:1])
        cfa = [Tile(ar, f"cfa{i}", S, BF16) for i in range(2)]
        for j in range(8):
            o_ = cfa[j % 2]
            for tt in range(NT):
                sl = slice(tt * 512, (tt + 1) * 512)
                d1 = lnt[0]
                d2 = lnt[1]
                TT(d1.ap, uc[j].ap[:, sl], csum.ap[:, sl], ALU.subtract, [uc[j].b, csum.b], [d1.b])
                TT(d1.ap, d1.ap, csq.ap[:, sl], ALU.mult, [d1.b, csq.b], [d1.b])
                ACT(d2.ap, d1.ap, AF.Tanh, [d1.b, hw.b], [d2.b], bias=hw.ap[:, o_hb + j: o_hb + j + 1], scale=hw.ap[:, o_hg + j: o_hg + j + 1])
                ACT(d1.ap, d1.ap, AF.Identity, [d1.b, par.b], [d1.b], bias=pcol("lnb", j, 1), scale=pcol("lng", j, 1))
                STT(o_.ap[:, sl], d2.ap, 1.0, d1.ap, ALU.add, ALU.mult, [d2.b, d1.b], [o_.b])
            Sd.dma("sp", cf_s[j], o_.ap, r=[o_.b], w=[cf_b[j]])
            if j == 0:
                DBG("cfa0", o_.ap[:, 0:512], [o_.b])
        for t in uc + [csum, csq] + cfa + lnt + tg:
            t.free()

        CKPT(3)

        wsAda[0] = WStream(worderAda2, "wbufAdb", 4096)
        gst = [Tile(ar, f"gst{i}", S, BF16) for i in range(2)]
        gi = 0
        for which, dst, dbufs in (("a", ga_s, ga_b), ("b", gb_s, gb_b)):
            for g in range(8):
                gidx = (0 if which == "a" else 8) + g
                wt = next_weight("win")
                for n in range(2):
                    t = gst[gi % 2]
                    gi += 1
                    for tt in range(NT):
                        pb = bank(MMB)
                        proj(wt, n * 128, 256, tt, pb)
                        ACT(t.ap[:, tt * 512:(tt + 1) * 512], ps[pb][:, :], AF.Tanh, [psb[pb]], [t.b], scale=0.5)
                    Sd.dma("sp", dst[g * 2 + n], t.ap, r=[t.b], w=[dbufs[g * 2 + n]])
                ada_mm(16 + 2 * gidx, 7)
                ada_mm(17 + 2 * gidx, 7)
        for t in gst:
            t.free()
        ada_fin(16, 48, 7, mod.bufs[1])
        wsAda[0].done()
        STT(gs2.ap, mod.ap[:, 64:80], 1.0, pcol("g2", 0, 16), ALU.add, ALU.mult, [mod.bufs[1], par.b], [gs2.b])
        TS(gt1q.ap, mod.ap[:, 32:48], 0.25, None, ALU.mult, None, [mod.bufs[1]], [gt1q.b])

        CKPT(4)
        raw = [Tile(ar, f"raw{i}", 4 + 512, F32) for i in range(2)]
        acc = [Tile(ar, f"acc{i}", 512, F32) for i in range(2)]
        tnt = [Tile(ar, f"tnt{i}", 512, F32) for i in range(2)]
        rsq = Tile(ar, "rsq", 512, F32)
        sqq = Tile(ar, "sqq", S, BF16)
        sqql = [Tile(ar, f"sqql{i}", 512, BF16) for i in range(2)]
        qkv = [Tile(ar, f"qkv{i}", S, BF16) for i in range(3)]
        zs2 = [Tile(ar, "zs", S, BF16)] * 2
        ogt = Tile(ar, "ogt", S, BF16)
        S32 = Tile(ar, "S32", 128, F32)
        Sbp = [Tile(ar, f"Sb{i}", 128, BF16) for i in range(2)]
        kbg = Tile(ar, "kbg", 1024, BF16)
        kdec = [Tile(ar, "kdec", 1024, BF16)] * 2
        wtm = Tile(ar, "wtm", 1024, BF16)
        vtm = Tile(ar, "vtm", 1024, BF16)
        coef = Tile(ar, "coef", 4 * 512, F32)
        Amat = Tile(ar, "Amat", 4 * 512, BF16)
        Xm = Tile(ar, "Xm", 512, F32)
        Xb = Tile(ar, "Xb", 512, BF16)
        attnTp = [Tile(ar, f"attnT{i}", 512, BF16) for i in range(2)]
        u0p = [Tile(ar, f"u0b{i}", 1024, BF16) for i in range(2)]
        KTp = [Tile(ar, f"KT{i}", 1024, BF16) for i in range(2)]
        Hnp = [Tile(ar, f"Hn{i}", 1024, BF16) for i in range(2)]
        qdecp = [Tile(ar, f"QpT{i}", 512, BF16) for i in range(2)]
        otile = Tile(ar, "otile", 512, F32)
        orstd = Tile(ar, "orstd", 512, F32)
        import os as _os
        POOLE = _os.environ.get("K_POOL", "dve")
        TEV = _os.environ.get("K_TEV", "dve")
        P1B = [int(c) for c in _os.environ.get("K_P1B", "0123")]
        SCB = [int(c) for c in _os.environ.get("K_SCB", "6")]
        PJB = [int(c) for c in _os.environ.get("K_PJB", "2345")]
        OB = 7
        ai_ = [0]
        ACTCP = lambda o, i, r, w: Sd.add("act", lambda e: e.activation(out=o, in_=i, func=AF.Copy), r, w, cost=(_fsz(o) + 300.0) / 1.2)

        acc3 = acc + [rsq]

        def gen_proj(h):
            zs = zs2[h % 2]
            items = [dict(ci=ci, tt=tt, i=ci * NT + tt) for ci in range(4) for tt in range(NT)]
            wts = {}

            def S1(it):
                ci, tt = it["ci"], it["tt"]
                if ci % 2 == 0 and tt == 0:
                    wts["cur"] = next_weight("win")
                it["pb"] = bank(PJB)
                proj(wts["cur"], (ci % 2) * 128, 256, tt, it["pb"])

            def S2(it):
                ci, tt, pb = it["ci"], it["tt"], it["pb"]
                sl = slice(tt * 512, (tt + 1) * 512)
                if ci == 3:
                    t_ = tnt[it["i"] % 2]
                    ACT(t_.ap, ps[pb][:, :], AF.Tanh, [psb[pb]], [t_.b], scale=0.5)
                    STT(zs.ap[:, sl], t_.ap, 1.0, ps[pb][:, :], ALU.add, ALU.mult, [t_.b, psb[pb]], [zs.b])
                    return
                wofs = o_dcw + (ci * 8 + h) * 4
                rw = raw[tt % 2]
                if tt == 0:
                    MEMSET(rw.ap[:, 0:4], 0.0, [rw.b])
                else:
                    pv_ = raw[(tt - 1) % 2]
                    ACTCP(rw.ap[:, 1:4], pv_.ap[:, 513:516], [pv_.b], [rw.b])
                ACTCP(rw.ap[:, 4:516], ps[pb][:, :], [psb[pb]], [rw.b])
                ac = acc3[it["i"] % 3]
                tn = tnt[it["i"] % 2]
                it["ac"] = ac
                if _os2.environ.get("K_CONV0"):
                    _WHATIF["scale"] = 0.25
                TS(ac.ap, rw.ap[:, 1:513], hw.ap[:, wofs:wofs + 1], None, ALU.mult, None, [rw.b, hw.b], [ac.b])
                for k in range(1, 4):
                    STT(ac.ap, rw.ap[:, 1 + k:513 + k], hw.ap[:, wofs + k:wofs + k + 1], ac.ap, ALU.mult, ALU.add,
                        [rw.b, hw.b, ac.b], [ac.b])
                _WHATIF["scale"] = 1.0
                ACT(tn.ap, ac.ap, AF.Tanh, [ac.b], [tn.b])
                if ci == 2:
                    STT(qkv[2].ap[:, sl], tn.ap, 1.0, ac.ap, ALU.add, ALU.mult, [tn.b, ac.b], [qkv[2].b])
                else:
                    STT(ac.ap, tn.ap, 1.0, ac.ap, ALU.add, ALU.mult, [tn.b, ac.b], [ac.b])
                    if ci == 0:
                        ACTCP(qkv[0].ap[:, sl], ac.ap, [ac.b], [qkv[0].b])

            def S3(it):
                ci, tt = it["ci"], it["tt"]
                if ci > 1:
                    return
                sl = slice(tt * 512, (tt + 1) * 512)
                ac = it["ac"]
                pq = bank(PJB)
                sumsq_accum(ac.ap, [ac.b], tt, True, True, pq)
                if ci == 0:
                    if tt == NT - 1:
                        ACTCP(sqql[h % 2].ap, ps[pq][:, :], [psb[pq]], [sqql[h % 2].b])
                    else:
                        ACTCP(sqq.ap[:, sl], ps[pq][:, :], [psb[pq]], [sqq.b])
                else:
                    RSQRT(ps[pq][:, :], ps[pq][:, :], [psb[pq]], [psb[pq]], epsb.ap[:, 0:1])
                    TT(qkv[1].ap[:, sl], ac.ap, ps[pq][:, :], ALU.mult, [ac.b, psb[pq]], [qkv[1].b])

            n_it = len(items)
            for t in range(n_it + 2):
                if t < n_it:
                    S1(items[t])
                if 0 <= t - 1 < n_it:
                    if items[t - 1]["ci"] == 3 and items[t - 1]["tt"] == 0:
                        yield "BARRIER"
                    S2(items[t - 1])
                if 0 <= t - 2 < n_it:
                    S3(items[t - 2])
                yield
            if h == 0:
                DBG("q0", qkv[0].ap[:, 0:512], [qkv[0].b])
                DBG("k0", qkv[1].ap[:, 0:512], [qkv[1].b])
                DBG("v0", qkv[2].ap[:, 0:512], [qkv[2].b])

        def gen_p1(h, tt):
            qT, kT, vT = qkv
            par_ = tt % 2
            attnT, u0, qdec, kd = attnTp[par_], u0p[par_], qdecp[par_], kdec[par_]
            KTb, Hnb = KTp[par_], Hnp[par_]
            tsl = slice(tt * 512, (tt + 1) * 512)
            n0 = tt * 8

            def tcol(T, c0, cn):
                base = (n0 + c0) * 8 + h
                return T[:, base: base + (cn - 1) * 8 + 1: 8]

            for half in range(2):
                pk = bank(P1B)
                pv = bank(P1B)
                for c in range(4):
                    cs = slice(tt * 512 + (half * 4 + c) * 64, tt * 512 + (half * 4 + c + 1) * 64)
                    MM(ps[pk][0:64, c * 128:(c + 1) * 128], kT.ap[:, cs], identb.ap, True, True, [kT.b, identb.b], [psb[pk]])
                    MM(ps[pv][0:64, c * 128:(c + 1) * 128], vT.ap[:, cs], identb.ap, True, True, [vT.b, identb.b], [psb[pv]])
                k3 = ps[pk][0:64, :].rearrange("p (c d) -> p c d", d=128)
                v3 = ps[pv][0:64, :].rearrange("p (c d) -> p c d", d=128)
                o1 = kbg.ap[0:64, half * 512:(half + 1) * 512].rearrange("p (c d) -> p c d", d=128)
                o2 = kd.ap[0:64, half * 512:(half + 1) * 512].rearrange("p (c d) -> p c d", d=128)
                o3 = vtm.ap[0:64, half * 512:(half + 1) * 512].rearrange("p (c d) -> p c d", d=128)
                TT(o1, k3, tcol(T_BEG, half * 4, 4).unsqueeze(2).to_broadcast([64, 4, 128]), ALU.mult, [psb[pk], tabs.b], [kbg.b])
                if TEV == "act":
                    for c in range(4):
                        ACT(o2[:, c, :], k3[:, c, :], AF.Identity, [psb[pk], tabs.b], [kd.b], scale=tcol(T_EDL, half * 4 + c, 1))
                    for c in range(4):
                        ACT(o3[:, c, :], v3[:, c, :], AF.Identity, [psb[pv], tabs.b], [vtm.b], scale=tcol(T_BETA, half * 4 + c, 1))
                else:
                    TT(o2, k3, tcol(T_EDL, half * 4, 4).unsqueeze(2).to_broadcast([64, 4, 128]), ALU.mult, [psb[pk], tabs.b], [kd.b])
                    TT(o3, v3, tcol(T_BETA, half * 4, 4).unsqueeze(2).to_broadcast([64, 4, 128]), ALU.mult, [psb[pv], tabs.b], [vtm.b])
                yield
            pkk = bank(P1B)
            pqk = bank(P1B)
            for c in range(8):
                cs = slice(tt * 512 + c * 64, tt * 512 + (c + 1) * 64)
                MM(ps[pkk][0:64, c * 64:(c + 1) * 64], kT.ap[:, cs], kT.ap[:, cs], True, True, [kT.b], [psb[pkk]])
                MM(ps[pqk][0:64, c * 64:(c + 1) * 64], kT.ap[:, cs], qT.ap[:, cs], True, True, [kT.b, qT.b], [psb[pqk]])
            yield
            pg = bank(P1B)
            MM(ps[pg][0:64, :], ccol("sel64", h * 64, 64, rows=8), R_GC[:, tsl], True, True, [con.b, rows.b], [psb[pg]])
            cZ, cAt, cA, cAT = [coef.ap[0:64, i * 512:(i + 1) * 512] for i in range(4)]
            cT = cAt
            r3 = lambda a_: a_.rearrange("p (c y) -> p c y", y=64)
            gcx = tcol(T_GC, 0, 8).unsqueeze(2).to_broadcast([64, 8, 64])
            btx = tcol(T_BETA, 0, 8).unsqueeze(2).to_broadcast([64, 8, 64])
            TT(r3(cZ), r3(ps[pg][0:64, :]), gcx, ALU.subtract, [psb[pg], tabs.b], [coef.b])
            mk = lambda nm: ccol(nm, 0, 64, rows=64).unsqueeze(1).to_broadcast([64, 8, 64])
            TT(r3(cZ), r3(cZ), mk("sgn"), ALU.mult, [coef.b, con.b], [coef.b])
            ACT(cZ, cZ, AF.Exp, [coef.b], [coef.b])
            yield
            pbr = bank(P1B)
            MM(ps[pbr][0:64, :], ccol("sel64", h * 64, 64, rows=8), R_BETA[:, tsl], True, True, [con.b, rows.b], [psb[pbr]])
            TT(r3(cAt), r3(cZ), mk("mge"), ALU.mult, [coef.b, con.b], [coef.b])
            TT(attnT.ap[0:64, :], ps[pqk][0:64, :], cAt, ALU.mult, [psb[pqk], coef.b], [attnT.b])
            TT(r3(cT), r3(cZ), mk("mlt"), ALU.mult, [coef.b, con.b], [coef.b], eng=POOLE)
            TT(r3(cA), r3(cT), btx, ALU.mult, [coef.b, tabs.b], [coef.b], eng=POOLE)
            yield
            TT(r3(cT), r3(cZ), mk("mgt"), ALU.mult, [coef.b, con.b], [coef.b], eng=POOLE)
            TT(cAT, cT, ps[pbr][0:64, :], ALU.mult, [coef.b, psb[pbr]], [coef.b])
            M = [Amat.ap[0:64, 0:512], Amat.ap[0:64, 512:1024]]
            N = [Amat.ap[0:64, 1024:1536], Amat.ap[0:64, 1536:2048]]
            TT(M[0], ps[pkk][0:64, :], cA, ALU.mult, [psb[pkk], coef.b], [Amat.b])
            TT(cT, ps[pkk][0:64, :], cAT, ALU.mult, [psb[pkk], coef.b], [coef.b])
            yield
            ACTCP(N[0], cT, [coef.b], [Amat.b])
            identbc = ccol("ident", 0, 64, rows=64).unsqueeze(1).to_broadcast([64, 8, 64])
            TT(r3(Xb.ap[0:64, :]), identbc, r3(cT), ALU.subtract, [con.b, coef.b], [Xb.b])
            TT(r3(Xm.ap[0:64, :]), identbc, r3(cT), ALU.subtract, [con.b, coef.b], [Xm.b], eng=POOLE)
            yield
            cur = 0
            for lvl in range(1, 6):
                nxt = 1 - cur
                pm = bank(P1B)
                for c in range(8):
                    cs = slice(c * 64, (c + 1) * 64)
                    MM(ps[pm][0:64, cs], N[cur][:, cs], M[cur][:, cs], True, True, [Amat.b], [psb[pm]])
                if lvl < 5:
                    pn = bank(P1B)
                    for c in range(8):
                        cs = slice(c * 64, (c + 1) * 64)
                        MM(ps[pn][0:64, cs], M[cur][:, cs], N[cur][:, cs], True, True, [Amat.b], [psb[pn]])
                ACTCP(M[nxt], ps[pm][0:64, :], [psb[pm]], [Amat.b])
                if lvl < 5:
                    ACTCP(N[nxt], ps[pn][0:64, :], [psb[pn]], [Amat.b])
                yield
                px = bank(P1B)
                for c in range(8):
                    cs = slice(c * 64, (c + 1) * 64)
                    MM(ps[px][0:64, cs], M[nxt][:, cs], Xb.ap[0:64, cs], True, True, [Amat.b, Xb.b], [psb[px]])
                TT(Xb.ap[0:64, :], Xm.ap[0:64, :], ps[px][0:64, :], ALU.add, [Xm.b, psb[px]], [Xb.b])
                if lvl < 5:
                    TT(Xm.ap[0:64, :], Xm.ap[0:64, :], ps[px][0:64, :], ALU.add, [Xm.b, psb[px]], [Xm.b])
                cur = nxt
                yield
            for half in range(2):
                pu = bank(P1B)
                pw = bank(P1B)
                for c in range(4):
                    cc = half * 4 + c
                    MM(ps[pu][0:64, c * 128:(c + 1) * 128], Xb.ap[0:64, cc * 64:(cc + 1) * 64],
                       vtm.ap[0:64, cc * 128:(cc + 1) * 128], True, True, [Xb.b, vtm.b], [psb[pu]])
                    MM(ps[pw][0:64, c * 128:(c + 1) * 128], Xb.ap[0:64, cc * 64:(cc + 1) * 64],
                       kbg.ap[0:64, cc * 128:(cc + 1) * 128], True, True, [Xb.b, kbg.b], [psb[pw]])
                ACTCP(u0.ap[0:64, half * 512:(half + 1) * 512], ps[pu][0:64, :], [psb[pu]], [u0.b])
                CP(wtm.ap[0:64, half * 512:(half + 1) * 512], ps[pw][0:64, :], [psb[pw]], [wtm.b])
                yield
            for half in range(2):
                pk_ = bank(P1B)
                ph_ = bank(P1B)
                for c in range(4):
                    cc = half * 4 + c
                    c128 = slice(cc * 128, (cc + 1) * 128)
                    MM(ps[pk_][:, c * 128:(c + 1) * 128], wtm.ap[0:64, c128], kd.ap[0:64, c128], True, True,
                       [wtm.b, kd.b], [psb[pk_]])
                    MM(ps[ph_][:, c * 128:(c + 1) * 128], kd.ap[0:64, c128], u0.ap[0:64, c128], True, True,
                       [kd.b, u0.b], [psb[ph_]])
                ACTCP(KTb.ap[:, half * 512:(half + 1) * 512], ps[pk_][:, :], [psb[pk_]], [KTb.b])
                ACT(Hnb.ap[:, half * 512:(half + 1) * 512], ps[ph_][:, :], AF.Identity, [psb[ph_]], [Hnb.b], scale=-1.0)
                yield
            p8 = bank(P1B)
            MM(ps[p8][:, :], ccol("sel128", h * 128, 128, rows=8), R_GC[:, tsl], True, True, [con.b, rows.b], [psb[p8]])
            ACT(ps[p8][:, :], ps[p8][:, :], AF.Exp, [psb[p8]], [psb[p8]])
            pq_ = bank(P1B)
            for c in range(8):
                MM(ps[pq_][:, c * 64:(c + 1) * 64], wtm.ap[0:64, c * 128:(c + 1) * 128], attnT.ap[0:64, c * 64:(c + 1) * 64],
                   True, True, [wtm.b, attnT.b], [psb[pq_]])
            qtmp = coef.ap[:, 0:512]
            TT(qtmp, qT.ap[:, tsl], ps[p8][:, :], ALU.mult, [qT.b, psb[p8]], [coef.b])
            TT(qdec.ap, qtmp, ps[pq_][:, :], ALU.subtract, [coef.b, psb[pq_]], [qdec.b])
            if h == 0 and tt == 0:
                DBG("attnT", attnT.ap[0:64, :], [attnT.b])
                DBG("Xm", Xm.ap[0:64, :], [Xm.b])
            yield

        def gen_scan(h, tt):
            par_ = tt % 2
            attnT, u0, qdec = attnTp[par_], u0p[par_], qdecp[par_]
            KTb, Hnb = KTp[par_], Hnp[par_]
            zs = zs2[h % 2]
            tsl = slice(tt * 512, (tt + 1) * 512)
            n0 = tt * 8
            for c in range(8):
                n = n0 + c
                Sc, Sn = Sbp[n % 2], Sbp[(n + 1) % 2]
                cs64 = slice(c * 64, (c + 1) * 64)
                c128 = slice(c * 128, (c + 1) * 128)
                pk = bank(SCB)
                MM(ps[pk][:, 0:128], KTb.ap[:, c128], Sc.ap, True, False, [KTb.b, Sc.b], [psb[pk]])
                MM(ps[pk][:, 0:128], identb.ap, Hnb.ap[:, c128], False, True, [identb.b, Hnb.b], [psb[pk]])
                MM(ps[OB][:, cs64], Sc.ap, qdec.ap[:, cs64], True, False, [Sc.b, qdec.b], [psb[OB]])
                MM(ps[OB][:, cs64], u0.ap[0:64, c128], attnT.ap[0:64, cs64], False, True, [u0.b, attnT.b], [psb[OB]])
                yield
                eg = egl.ap[:, h * NCH + n: h * NCH + n + 1]
                STT(Sn.ap, S32.ap, eg, ps[pk][:, 0:128], ALU.mult, ALU.subtract, [S32.b, egl.b, psb[pk]], [Sn.b])
                STT(S32.ap, S32.ap, eg, ps[pk][:, 0:128], ALU.mult, ALU.subtract, [S32.b, egl.b, psb[pk]], [S32.b])
                yield
            ACTCP(otile.ap, ps[OB][:, :], [psb[OB]], [otile.b])
            pb = bank(SCB)
            sumsq_accum(ps[OB][:, :], [psb[OB]], tt, True, True, pb)
            yield
            sq_t, sq_ap = (sqql[h % 2], sqql[h % 2].ap) if tt == NT - 1 else (sqq, sqq.ap[:, tsl])
            STT(orstd.ap, sq_ap, 128.0 * 128.0 * EPS, ps[pb][:, :], ALU.mult, ALU.add, [sq_t.b, psb[pb]], [orstd.b])
            RSQRT(orstd.ap, orstd.ap, [orstd.b], [orstd.b], epsb.ap[:, 2:3], 1.0 / 128)
            TT(otile.ap, otile.ap, orstd.ap, ALU.mult, [otile.b, orstd.b], [otile.b])
            STT(ogt.ap[:, tsl], otile.ap, pcol("dng", 0, 1), zs.ap[:, tsl], ALU.mult, ALU.mult, [otile.b, par.b, zs.b], [ogt.b])
            if tt == NT - 1:
                Sd.dma("sp", og_s[h], ogt.ap, r=[ogt.b], w=[og_b[h]])
            yield

        def run_gen(g):
            if g is not None:
                for _ in g:
                    pass

        def interleave(ga, gb, ra=1, rb=1):
            gens = [[ga, ra, False], [gb, rb, False]]
            while not (gens[0][2] and gens[1][2]):
                for gi_, gsl in enumerate(gens):
                    for _ in range(gsl[1]):
                        if not gsl[2]:
                            try:
                                v = next(gsl[0])
                                if v == "BARRIER":
                                    oth = gens[1 - gi_]
                                    if not oth[2]:
                                        run_gen(oth[0])
                                        oth[2] = True
                            except StopIteration:
                                gsl[2] = True

        pending = None
        for h in range(H):
            gp = gen_proj(h)
            if pending is not None:
                interleave(pending, gp, 1, 1)
                pending = None
            else:
                run_gen(gp)
            MEMSET(S32.ap, 0.0, [S32.b])
            MEMSET(Sbp[0].ap, 0.0, [Sbp[0].b])
            run_gen(gen_p1(h, 0))
            for tt in range(NT):
                gs = gen_scan(h, tt)
                if tt + 1 < NT:
                    interleave(gs, gen_p1(h, tt + 1), int(_osx.environ.get("K_IA", "2")), int(_osx.environ.get("K_IB", "3")))
                else:
                    pending = gs
        run_gen(pending)
        for t in raw + acc + tnt + qkv + zs2[:1] + Sbp + [sqq] + sqql + kdec[:1] + attnTp + u0p + KTp + Hnp + qdecp + \
                [rsq, ogt, S32, kbg, wtm, vtm, coef, Amat, Xm, Xb, otile, orstd]:
            t.free()
        for t in hn + [rows, tabs, egl, tot, wba]:
            t.free()

        CKPT(5)
        ogr = [Tile(ar, f"ogr{k}", S, BF16) for k in range(8)]
        cfr = [Tile(ar, f"cfr{k}", S, BF16) for k in range(8)]
        for k in range(8):
            Sd.dma("sp", ogr[k].ap, og_s[k], r=[og_b[k]], w=[ogr[k].b])
            Sd.dma("sp", cfr[k].ap, cf_s[k], r=[cf_b[k]], w=[cfr[k].b])
        merged = [Tile(ar, f"mg{k}", S, BF16) for k in range(KC)]
        gl = [Tile(ar, f"gl{i}", S, BF16) for i in range(4)]
        mt = [Tile(ar, f"mt{i}", 512, F32) for i in range(4)]
        mi = 0
        for gq in range(8):
            wab = next_weight("wab")
            for n in range(2):
                nn = gq * 2 + n
                ta = gl[(nn % 2) * 2]
                tb = gl[(nn % 2) * 2 + 1]
                Sd.dma("sp", ta.ap, ga_s[nn], r=[ga_b[nn]], w=[ta.b])
                Sd.dma("sp", tb.ap, gb_s[nn], r=[gb_b[nn]], w=[tb.b])
                for tt in range(NT):
                    sl = slice(tt * 512, (tt + 1) * 512)
                    pa = bank(MMB)
                    for kc in range(8):
                        MM(ps[pa][:, :], wab.ap[:, kc * 256 + n * 128: kc * 256 + (n + 1) * 128], ogr[kc].ap[:, sl],
                           kc == 0, kc == 7, [wab.b, ogr[kc].b], [psb[pa]])
                    pb = bank(MMB)
                    for kc in range(8):
                        MM(ps[pb][:, :], wab.ap[:, (8 + kc) * 256 + n * 128: (8 + kc) * 256 + (n + 1) * 128], cfr[kc].ap[:, sl],
                           kc == 0, kc == 7, [wab.b, cfr[kc].b], [psb[pb]])
                    m1 = mt[mi % 4]
                    m2_ = mt[(mi + 1) % 4]
                    mi += 2
                    STT(m1.ap, ta.ap[:, sl], 1.0, ps[pa][:, :], ALU.add, ALU.mult, [ta.b, psb[pa]], [m1.b])
                    STT(m2_.ap, tb.ap[:, sl], 1.0, ps[pb][:, :], ALU.add, ALU.mult, [tb.b, psb[pb]], [m2_.b])
                    TT(merged[nn].ap[:, sl], m1.ap, m2_.ap, ALU.add, [m1.b, m2_.b], [merged[nn].b])
        for t in ogr + cfr + gl:
            t.free()
        x1t = [Tile(ar, f"x1t{i}", S, F32) for i in range(2)]
        alloc_xl()
        rstd = Tile(ar, "rstd2", S, F32)
        for g in range(8):
            wt = next_weight("wout")
            for n in range(2):
                nn = g * 2 + n
                xt = load_chunk(xT_d[nn * 128:(nn + 1) * 128, :], [])
                x1 = x1t[nn % 2]
                for tt in range(NT):
                    sl = slice(tt * 512, (tt + 1) * 512)
                    pb = bank(MMB)
                    for kc in range(KC):
                        MM(ps[pb][:, :], wt.ap[:, kc * 256 + n * 128: kc * 256 + (n + 1) * 128], merged[kc].ap[:, sl],
                           kc == 0, kc == KC - 1, [wt.b, merged[kc].b], [psb[pb]])
                    flush_deferred(1)
                    STT(x1.ap[:, sl], ps[pb][:, :], gt1q.ap[:, nn:nn + 1], xt.ap[:, sl], ALU.mult, ALU.add,
                        [psb[pb], gt1q.b, xt.b], [x1.b])
                    sumsq_accum(x1.ap[:, sl], [x1.b], tt, nn == 0, nn == 15, SB[tt], defer=True)
                Sd.dma("sp", x1_s[nn], x1.ap, r=[x1.b], w=[x1_b[nn]])
                if nn == 0:
                    DBG("x1_0", x1.ap[:, 0:512], [x1.b])
        flush_deferred()
        for tt in range(NT):
            rstd_from(SB[tt], rstd.ap[:, tt * 512:(tt + 1) * 512], [rstd.b], 1.0 / D)
        for t in merged + x1t + mt:
            t.free()

        CKPT(6)
        hn = [Tile(ar, f"hn2_{k}", S, BF16) for k in range(KC)]
        free_xl()
        alloc_xl(6)
        norm_modulate(lambda k: x1_s[k], lambda k: [x1_b[k]], gs2, 48, mod.bufs[1])
        free_xl()
        alloc_xl(2)
        FFB = [0, 1, 2, 3, 4, 5, 6, 7]
        graw = [Tile(ar, f"graw{i}", 4 + S, F32) for i in range(2)]
        for t in graw:
            MEMSET(t.ap[:, 0:4], 0.0, [t.b])
        hch = [Tile(ar, f"hch{i}", S, BF16) for i in range(2)]
        fa = [Tile(ar, f"fa{i}", 512, F32) for i in range(2)]
        ft = [Tile(ar, f"ft{i}", 512, F32) for i in range(2)]
        fi = 0
        for f in range(NF):
            wt = next_weight("wup")
            gr = graw[f % 2]
            hc = hch[f % 2]
            wofs = o_fcw + f * 3
            for tt in range(NT):
                sl = slice(tt * 512, (tt + 1) * 512)
                pg = bank(FFB)
                proj(wt, 0, 256, tt, pg)
                pu = bank(FFB)
                proj(wt, 128, 256, tt, pu)
                Sd.add("act", lambda e, o=gr.ap[:, 4 + tt * 512: 4 + (tt + 1) * 512], i=ps[pg][:, :]: e.activation(out=o, in_=i, func=AF.Copy),
                       [psb[pg]], [gr.b])
                a_ = fa[fi % 2]
                t_ = ft[fi % 2]
                fi += 1
                c0 = tt * 512 + 2
                TS(a_.ap, gr.ap[:, c0:c0 + 512], hw.ap[:, wofs:wofs + 1], None, ALU.mult, None, [gr.b, hw.b], [a_.b])
                for k in range(1, 3):
                    STT(a_.ap, gr.ap[:, c0 + k:c0 + k + 512], hw.ap[:, wofs + k:wofs + k + 1], a_.ap, ALU.mult, ALU.add,
                        [gr.b, hw.b, a_.b], [a_.b])
                ACT(t_.ap, a_.ap, AF.Tanh, [a_.b], [t_.b])
                STT(a_.ap, t_.ap, 1.0, a_.ap, ALU.add, ALU.mult, [t_.b, a_.b], [a_.b])
                TT(hc.ap[:, sl], a_.ap, ps[pu][:, :], ALU.mult, [a_.b, psb[pu]], [hc.b])
            Sd.dma("sp", h_s[f], hc.ap, r=[hc.b], w=[h_b[f]])
            if f == 0:
                DBG("h0", hc.ap[:, 0:512], [hc.b])
        for t in hn + graw + hch + fa + ft:
            t.free()
        CKPT(7)
        wsA.done()
        wsB = WStream(worderB, "wbufB", NF * 128, nb=3)
        cur_ws[0] = wsB
        free_xl()
        HS = min(S, 1024)
        NHT = HS // 512
        D2B = [0, 1, 2, 3, 6, 7]
        x1l = [Tile(ar, f"x1l{i}", HS, F32) for i in range(3)]
        x2t = [Tile(ar, f"x2t{i}", HS, F32) for i in range(3)]
        NFB = int(_osx.environ.get("K_NFB", "4"))
        fxl = [Tile(ar, f"fxl{i}", HS, F32) for i in range(NFB)]
        fot = [Tile(ar, f"fot{i}", HS, F32) for i in range(NFB)]
        cnt = 0
        for half in range(S // HS):
            hs0 = half * HS
            hh = [Tile(ar, f"hh{half}_{f}", HS, BF16) for f in range(NF)]
            for f in range(NF):
                Sd.dma("sp", hh[f].ap, h_s[f][:, hs0:hs0 + HS], r=[h_b[f]], w=[hh[f].b])
            for n in range(16):
                wt = next_weight("wdn")
                xt = x1l[cnt % 3]
                x2 = x2t[cnt % 3]
                cnt += 1
                Sd.dma("sp", xt.ap, x1_s[n][:, hs0:hs0 + HS], r=[x1_b[n]], w=[xt.b])
                for tt in range(NHT):
                    sl = slice(tt * 512, (tt + 1) * 512)
                    pb = bank(D2B)
                    for f in range(NF):
                        MM(ps[pb][:, :], wt.ap[:, f * 128:(f + 1) * 128], hh[f].ap[:, sl], f == 0, f == NF - 1,
                           [wt.b, hh[f].b], [psb[pb]])
                    flush_deferred(1)
                    STT(x2.ap[:, sl], ps[pb][:, :], mod.ap[:, 80 + n: 81 + n], xt.ap[:, sl], ALU.mult, ALU.add,
                        [psb[pb], mod.bufs[1], xt.b], [x2.b])
                    sumsq_accum(x2.ap[:, sl], [x2.b], tt, n == 0, n == 15, SB[tt], defer=True)
                Sd.dma("sp", x2_s[n][:, hs0:hs0 + HS], x2.ap, r=[x2.b], w=[x2_b[n]])
            flush_deferred()
            for tt in range(NHT):
                rstd_from(SB[tt], rstd.ap[:, hs0 + tt * 512: hs0 + (tt + 1) * 512], [rstd.b], 1.0 / D)
            for n in range(16):
                xt = fxl[n % NFB]
                Sd.dma("sp", xt.ap, x2_s[n][:, hs0:hs0 + HS], r=[x2_b[n]], w=[xt.b])
                o_ = fot[n % NFB]
                STT(o_.ap, xt.ap, pcol("gf", n, 1), rstd.ap[:, hs0:hs0 + HS], ALU.mult, ALU.mult, [xt.b, par.b, rstd.b], [o_.b])
                Sd.out_dma.append(Sd.dma("sp", out_d[n * 128:(n + 1) * 128, hs0:hs0 + HS], o_.ap, r=[o_.b]))
            for t in hh:
                t.free()

        wsB.done()
    except _Stop:
        pass
    print("ops:", Sd.count(), "arena peak KiB:", ar.peak / 1024)
    if RESCHED:
        est = Sd.reschedule()
        print("rescheduled; simulated makespan us:", est / 1e3)
    Sd.emit(nc, st, final_wait_ops=Sd.out_dma)
    st.close()
    return nc, PC, CC


_CACHE = {}


def _run(inputs, S, FF, dbg=(), ncores=8, stage=99):
    inp = {k: np.asarray(v, np.float32) for k, v in inputs.items()}
    key = (S, FF, tuple((n_, tuple(s_)) for n_, s_ in dbg), stage)
    if key not in _CACHE:
        _CACHE[key] = build_program(S, FF, dbg, stage)
    nc, PC, CC = _CACHE[key]
    W = _prep_weights(inp, FF)
    consts = _const_cols(S).build()
    in_maps = []
    for b in range(ncores):
        m = {"xT": np.ascontiguousarray(inp["x"][b].T), "params": _param_cols(inp, b, FF).build(), "consts": consts,
             "seg": _seg(S)}
        m.update(W)
        in_maps.append(m)
    res = run_bass_kernel_spmd(nc, in_maps, core_ids=list(range(ncores)))
    return res


def kernel(**inputs):
    S = inputs["x"].shape[1]
    FF = inputs["ffn_w_down"].shape[1]
    B = inputs["x"].shape[0]
    res = _run(inputs, S, FF, ncores=B)
    out = np.stack([np.ascontiguousarray(res.results[b]["outT"].T) for b in range(B)], 0)
    return out.astype(np.float32)
```
